# Optimizing a Trainium2 kernel written in Bass

```python
import math
import jax, jax.numpy as jnp
from jax import lax
import numpy as np

D_MODEL = 4096
BATCH = 2
SEQ = 4096
DEPTH = 1

MIX_WIDTH = D_MODEL
A_HEADS = 16
NOPE_DIM = 128
ROPE_DIM = 64
V_DIM = 128
A_WIDTH = A_HEADS * V_DIM
Q_LORA = 1024
KV_LORA = 512
ROPE_THETA = 10000.0
Q_BLOCK = 128
G_WIDTH = MIX_WIDTH - A_WIDTH
G_HEADS = 16
G_HEAD_DIM = G_WIDTH // G_HEADS
CHUNK = 128
D_FF = ((8 * D_MODEL // 3 + 255) // 256) * 256
IN_WIDTH = Q_LORA + KV_LORA + ROPE_DIM + 2 * G_WIDTH
EPS = 1e-6

kernel_name = "hybrid_mla_gmlp_sandwich_block"


def rms_norm(x, g):
    xf = x.astype(jnp.float32)
    y = xf * lax.rsqrt(jnp.mean(xf * xf, axis=-1, keepdims=True) + EPS)
    return (y * g.astype(jnp.float32)).astype(x.dtype)


def layer_norm(x, g, b):
    xf = x.astype(jnp.float32)
    mu = jnp.mean(xf, axis=-1, keepdims=True)
    xc = xf - mu
    y = xc * lax.rsqrt(jnp.mean(xc * xc, axis=-1, keepdims=True) + EPS)
    return (y * g.astype(jnp.float32) + b.astype(jnp.float32)).astype(x.dtype)


def rope_tables(positions, dtype):
    inv_freq = 1.0 / (ROPE_THETA ** (jnp.arange(0, ROPE_DIM, 2, dtype=jnp.float32) / ROPE_DIM))
    ang = positions.astype(jnp.float32)[..., None] * inv_freq
    return jnp.cos(ang).astype(dtype), jnp.sin(ang).astype(dtype)


def apply_rope(t, cos, sin):
    half = t.shape[-1] // 2
    t1, t2 = t[..., :half], t[..., half:]
    return jnp.concatenate([t1 * cos - t2 * sin, t2 * cos + t1 * sin], axis=-1)


def mla_attention(q_nope, q_rope, k_nope, k_rope, v):
    B, S, H, _ = q_nope.shape
    nb = S // Q_BLOCK
    scale = 1.0 / math.sqrt(NOPE_DIM + ROPE_DIM)
    qn = q_nope.reshape(B, nb, Q_BLOCK, H, NOPE_DIM).transpose(1, 0, 2, 3, 4)
    qr = q_rope.reshape(B, nb, Q_BLOCK, H, ROPE_DIM).transpose(1, 0, 2, 3, 4)

    def block(args):
        qn_b, qr_b = args
        s = (jnp.einsum('bqhd,bkhd->bhqk', qn_b, k_nope, preferred_element_type=jnp.float32)
             + jnp.einsum('bqhr,bkr->bhqk', qr_b, k_rope, preferred_element_type=jnp.float32))
        p = jax.nn.softmax(s * scale, axis=-1).astype(v.dtype)
        return jnp.einsum('bhqk,bkhd->bqhd', p, v)

    out = lax.map(block, (qn, qr))
    return out.transpose(1, 0, 2, 3, 4).reshape(B, S, H * V_DIM)


def spatial_gating(u, v, v_ln_g, v_ln_b, w_s, b_s):
    B, S, _ = v.shape
    v = layer_norm(v, v_ln_g, v_ln_b)
    vc = v.reshape(B, S // CHUNK, CHUNK, G_HEADS, G_HEAD_DIM)
    s = jnp.einsum('gpq,bcqgd->bcpgd', w_s, vc) + jnp.transpose(b_s)[None, None, :, :, None]
    return u * s.reshape(B, S, G_WIDTH)


def setup_inputs(seed: int = 0) -> dict:
    key = jax.random.key(seed)
    ks = jax.random.split(key, 24)
    f32 = jnp.float32
    L = DEPTH

    def nrm(k, shape, fan_in):
        return jax.random.normal(k, shape, f32) * (fan_in ** -0.5)

    def gain(k, n):
        return 1.0 + 0.02 * jax.random.normal(k, (L, n), f32)

    x = jax.random.normal(ks[0], (BATCH, SEQ, D_MODEL), f32)
    offs = jax.random.randint(ks[1], (BATCH, 1), 0, SEQ, dtype=jnp.int32)
    positions = jnp.arange(SEQ, dtype=jnp.int32)[None, :] + offs
    return {
        "x": x,
        "positions": positions,
        "pre_mix_norm": gain(ks[2], D_MODEL),
        "w_in": nrm(ks[3], (L, D_MODEL, IN_WIDTH), D_MODEL),
        "q_norm": gain(ks[4], Q_LORA),
        "kv_norm": gain(ks[5], KV_LORA),
        "w_uq": nrm(ks[6], (L, Q_LORA, A_HEADS * (NOPE_DIM + ROPE_DIM)), Q_LORA),
        "w_ukv": nrm(ks[7], (L, KV_LORA, A_HEADS * (NOPE_DIM + V_DIM)), KV_LORA),
        "v_ln_gain": gain(ks[8], G_WIDTH),
        "v_ln_bias": 0.02 * jax.random.normal(ks[9], (L, G_WIDTH), f32),
        "w_spatial": nrm(ks[10], (L, G_HEADS, CHUNK, CHUNK), CHUNK),
        "b_spatial": 1.0 + 0.1 * jax.random.normal(ks[11], (L, G_HEADS, CHUNK), f32),
        "attn_out_norm": gain(ks[12], A_WIDTH),
        "gmlp_out_norm": gain(ks[13], G_WIDTH),
        "w_out": nrm(ks[14], (L, MIX_WIDTH, D_MODEL), MIX_WIDTH),
        "post_mix_norm": gain(ks[15], D_MODEL),
        "pre_ffn_norm": gain(ks[16], D_MODEL),
        "w_gate": nrm(ks[17], (L, D_MODEL, D_FF), D_MODEL),
        "w_up": nrm(ks[18], (L, D_MODEL, D_FF), D_MODEL),
        "w_down": nrm(ks[19], (L, D_FF, D_MODEL), D_FF),
        "post_ffn_norm": gain(ks[20], D_MODEL),
    }


def reference(x, positions, pre_mix_norm, w_in, q_norm, kv_norm, w_uq, w_ukv,
              v_ln_gain, v_ln_bias, w_spatial, b_spatial, attn_out_norm, gmlp_out_norm,
              w_out, post_mix_norm, pre_ffn_norm, w_gate, w_up, w_down, post_ffn_norm):
    B, S, _ = x.shape
    cos, sin = rope_tables(positions, x.dtype)
    splits = np.cumsum([Q_LORA, KV_LORA, ROPE_DIM, G_WIDTH]).tolist()

    for l in range(DEPTH):
        xn = rms_norm(x, pre_mix_norm[l])
        proj = jnp.einsum('bsd,de->bse', xn, w_in[l])
        q_c, kv_c, k_rope, u, v = jnp.split(proj, splits, axis=-1)

        q = jnp.einsum('bsr,re->bse', rms_norm(q_c, q_norm[l]), w_uq[l])
        q = q.reshape(B, S, A_HEADS, NOPE_DIM + ROPE_DIM)
        q_nope = q[..., :NOPE_DIM]
        q_rope = apply_rope(q[..., NOPE_DIM:], cos[:, :, None, :], sin[:, :, None, :])
        kv = jnp.einsum('bsr,re->bse', rms_norm(kv_c, kv_norm[l]), w_ukv[l])
        kv = kv.reshape(B, S, A_HEADS, NOPE_DIM + V_DIM)
        k_nope, v_a = kv[..., :NOPE_DIM], kv[..., NOPE_DIM:]
        k_rope = apply_rope(k_rope, cos, sin)
        a_out = mla_attention(q_nope, q_rope, k_nope, k_rope, v_a)

        u = jax.nn.gelu(u)
        v = jax.nn.gelu(v)
        g_out = spatial_gating(u, v, v_ln_gain[l], v_ln_bias[l], w_spatial[l], b_spatial[l])

        mixed = jnp.concatenate([rms_norm(a_out, attn_out_norm[l]),
                                 rms_norm(g_out, gmlp_out_norm[l])], axis=-1)
        mix_out = jnp.einsum('bse,ed->bsd', mixed, w_out[l])
        x = x + rms_norm(mix_out, post_mix_norm[l])

        hn = rms_norm(x, pre_ffn_norm[l])
        gate = jnp.einsum('bsd,df->bsf', hn, w_gate[l])
        up = jnp.einsum('bsd,df->bsf', hn, w_up[l])
        ffn = jnp.einsum('bsf,fd->bsd', jax.nn.silu(gate) * up, w_down[l])
        x = x + rms_norm(ffn, post_ffn_norm[l])

    return x
```

```python
import math
from contextlib import ExitStack

import numpy as np

import concourse.bass as bass
import concourse.mybir as mybir
from concourse.bass_utils import run_bass_kernel_spmd

F32 = mybir.dt.float32
BF16 = mybir.dt.bfloat16
I32 = mybir.dt.int32
AF = mybir.ActivationFunctionType
ALU = mybir.AluOpType
EPS = 1e-6
PI = math.pi


class Cfg:
    def __init__(self, D=4096, SEQ=4096, DFF=11008, QL=1024, KVL=512, AH=16, GH=16,
                 TILE_BLK=32, NRING=4):
        self.D, self.SEQ, self.DFF, self.QL, self.KVL, self.AH, self.GH = D, SEQ, DFF, QL, KVL, AH, GH
        self.T = 512
        self.NH = 2
        self.OWN = self.T * self.NH
        self.DC = D // 128
        self.QC = QL // 128
        self.KC = KVL // 128
        self.FC = DFF // 128
        self.GW = GH * 128
        self.AW = AH * 128
        assert self.AW + self.GW == D
        self.NBLK = SEQ // 512
        self.NKT = SEQ // 128
        self.TB = TILE_BLK
        self.TCOLS = TILE_BLK * 128
        self.NRING = NRING
        self.IN_W = QL + KVL + 64 + 2 * self.GW
        self.ARENA_BYTES = 206 * 1024
        self.FAST_RECIP = False
        self.POOL_Z = False
        self.DEFER_ATT = False
        self.DEFER_A = True


def ksplit(kc_total, tb):
    out, s = [], 0
    while s < kc_total:
        n = min(tb, kc_total - s)
        out.append((s, n))
        s += n
    return out


def build_wstream(C, w_in, w_uq, w_ukv, w_out, w_gate, w_up, w_down):
    tiles = []
    index = {}

    def add(name, arr2d):
        t = np.zeros((128, C.TCOLS), np.float32)
        t[:, :arr2d.shape[1]] = arr2d
        index.setdefault(name, []).append(len(tiles))
        tiles.append(t)

    def oc_major(W):
        K, N = W.shape
        return W.reshape(K // 128, 128, N // 128, 128).transpose(2, 1, 0, 3)

    def add_linear(name, W):
        A = oc_major(W)
        OC, _, KCn, _ = A.shape
        if KCn >= C.TB:
            for oc in range(OC):
                for (s, n) in ksplit(KCn, C.TB):
                    add(name, A[oc, :, s:s + n, :].reshape(128, n * 128))
        else:
            per = C.TB // KCn
            for o0 in range(0, OC, per):
                blk = A[o0:o0 + per]
                add(name, blk.transpose(1, 0, 2, 3).reshape(128, -1))

    QL, KVL, GW = C.QL, C.KVL, C.GW
    c_q = w_in[:, :QL]
    c_kv = w_in[:, QL:QL + KVL]
    c_r = w_in[:, QL + KVL:QL + KVL + 64]
    c_u = w_in[:, QL + KVL + 64:QL + KVL + 64 + GW]
    c_v = w_in[:, QL + KVL + 64 + GW:]
    c_rsw = np.concatenate([c_r[:, 32:], c_r[:, :32]], axis=1)
    add_linear("kv", np.concatenate([c_kv, c_r, c_rsw], axis=1))
    ukv = w_ukv.reshape(KVL, C.AH, 256)
    add_linear("ukvk", np.ascontiguousarray(ukv[:, :, :128]).reshape(KVL, C.AH * 128))
    Vp = np.ascontiguousarray(ukv[:, :, 128:]).reshape(KVL // 128, 128, C.AH * 128)
    Vp = Vp.transpose(1, 0, 2).reshape(128, -1)
    for s in range(0, Vp.shape[1], C.TCOLS):
        add("ukvv", Vp[:, s:s + C.TCOLS])
    add_linear("q", c_q)
    add_linear("v", c_v)
    add_linear("u", c_u)
    uq = w_uq.reshape(QL, C.AH, 192)
    uqp = np.concatenate([uq[:, :, :128], uq[:, :, 128:192], uq[:, :, 160:192], uq[:, :, 128:160]], axis=2)
    add_linear("uq", uqp.reshape(QL, C.AH * 256))
    add_linear("out", w_out)
    Ag, Au = oc_major(w_gate), oc_major(w_up)
    for fc in range(C.FC):
        for (s, n) in ksplit(C.DC, C.TB):
            add("gu", Ag[fc, :, s:s + n, :].reshape(128, n * 128))
        for (s, n) in ksplit(C.DC, C.TB):
            add("gu", Au[fc, :, s:s + n, :].reshape(128, n * 128))
    add_linear("down", w_down)
    return np.stack(tiles, 0), index


class Buf:
    __slots__ = ("name", "w", "r", "dsem", "dcnt", "dlast")

    def __init__(self, name):
        self.name = name
        self.w = None
        self.r = []
        self.dsem = None
        self.dcnt = 0
        self.dlast = None


class Eng:
    def __init__(self, raw, sem, name):
        self.raw, self.sem, self.name = raw, sem, name
        self.cnt = 0
        self.seen = {}


class K:
    def __init__(self, nc, es):
        self.nc, self.es = nc, es
        self.nsem = 0

        def mk(raw, name):
            return Eng(raw, self.newsem("e_" + name), name)
        self.pe = mk(nc.tensor, "pe")
        self.act = mk(nc.scalar, "act")
        self.dve = mk(nc.vector, "dve")
        self.pool = mk(nc.gpsimd, "pool")
        self.sp = mk(nc.sync, "sp")
        self.nwait = 0
        self.dsems = {}

    def newsem(self, name):
        self.nsem += 1
        return self.es.enter_context(self.nc.semaphore(name))

    def wait(self, eng, ev):
        if ev is None:
            return
        sem, val = ev
        key = id(sem)
        if eng is self.pe and sem is self.pe.sem:
            return
        if eng.seen.get(key, 0) >= val:
            return
        eng.raw.wait_ge(sem, val)
        eng.seen[key] = val
        self.nwait += 1

    def deps(self, eng, reads, writes):
        for b in reads:
            self.wait(eng, b.w)
        for b in writes:
            self.wait(eng, b.w)
            for r in b.r:
                self.wait(eng, r)

    def commit(self, ev, reads, writes):
        for b in reads:
            b.r.append(ev)
        for b in writes:
            b.w = ev
            b.r = []

    def op(self, eng, fn, reads=(), writes=()):
        self.deps(eng, reads, writes)
        ins = fn()
        eng.cnt += 1
        ins.then_inc(eng.sem, 1)
        ev = (eng.sem, eng.cnt)
        self.commit(ev, reads, writes)
        return ins

    def group(self, eng, fns, reads=(), writes=()):
        self.deps(eng, reads, writes)
        ins = None
        for fn in fns:
            ins = fn()
        eng.cnt += 1
        ins.then_inc(eng.sem, 1)
        ev = (eng.sem, eng.cnt)
        self.commit(ev, reads, writes)
        return ins

    def dma(self, q, out, in_, sb, reads=(), writes=()):
        if sb.dsem is None:
            if sb.name not in self.dsems:
                self.dsems[sb.name] = [self.newsem("d_" + sb.name), 0, None]
            sb.dsem = self.dsems[sb.name]
        rec = sb.dsem
        self.deps(q, reads, writes)
        self.wait(q, rec[2])
        ins = q.raw.dma_start(out=out, in_=in_)
        rec[1] += 16
        ins.then_inc(rec[0], 16)
        ev = (rec[0], rec[1])
        rec[2] = ev
        sb.dlast = ev
        self.commit(ev, reads, writes)
        return ins


DEAD = {}


def retire(bufs):
    for b in bufs:
        evs = list(b.r)
        if b.w is not None:
            evs.append(b.w)
        if b.dlast is not None:
            evs.append(b.dlast)
        for (sem, val) in evs:
            key = id(sem)
            if key not in DEAD or DEAD[key][1] < val:
                DEAD[key] = (sem, val)


def fresh(bufs):
    evs = list(DEAD.values())
    for b in bufs:
        b.r = list(evs) + b.r
    return bufs


def alias_after(new_bufs, old_bufs):
    evs = []
    for b in old_bufs:
        if b.w is not None:
            evs.append(b.w)
        evs.extend(b.r)
    for b in new_bufs:
        b.r = list(evs) + b.r


class Arena:
    def __init__(self, nc, es, nbytes):
        self.t = es.enter_context(nc.sbuf_tensor("arena", [128, nbytes // 2], BF16))
        self.free_list = [(0, nbytes)]
        self.peak = 0
        self.used = 0

    def alloc(self, n):
        n = (n + 63) // 64 * 64
        for i, (o, sz) in enumerate(self.free_list):
            if sz >= n:
                if sz == n:
                    self.free_list.pop(i)
                else:
                    self.free_list[i] = (o + n, sz - n)
                self.used += n
                self.peak = max(self.peak, self.used)
                return o, n
        raise MemoryError(f"SBUF arena exhausted: need {n}, free {self.free_list}")

    def free(self, o, n):
        self.used -= n
        fl = sorted(self.free_list + [(o, n)])
        out = []
        for (a, b) in fl:
            if out and out[-1][0] + out[-1][1] == a:
                out[-1] = (out[-1][0], out[-1][1] + b)
            else:
                out.append((a, b))
        self.free_list = out

    def view(self, name, shape, dt, stack):
        P = shape[0]
        cols = 1
        for d in shape[1:]:
            cols *= d
        esz = 2 if dt == BF16 else 4
        o, n = self.alloc(cols * esz)
        stack.callback(self.free, o, n)
        v = self.t[0:P, o // 2:(o + cols * esz) // 2]
        if dt != BF16:
            v = v.bitcast(dt)
        if len(shape) == 3:
            v = v.rearrange("p (a b) -> p a b", a=shape[1])
        return v


def build_program(C, windex, debug=False):
    DEAD.clear()
    nc = bass.Bass("TRN2", target_bir_lowering=False)
    T, DC, QC, KC, FC, AH, GH = C.T, C.DC, C.QC, C.KC, C.FC, C.AH, C.GH
    SEQ, OWN = C.SEQ, C.OWN
    NT = sum(len(v) for v in windex.values())
    scr_kind = "ExternalOutput"
    assert DC == C.TB

    xT = nc.dram_tensor("xT", [C.D, SEQ], F32, kind="ExternalInput").ap()
    pos = nc.dram_tensor("pos", [1, SEQ], I32, kind="ExternalInput").ap()
    wst = nc.dram_tensor("wstream", [NT, 128, C.TCOLS], F32, kind="ExternalInput").ap()
    gcols = nc.dram_tensor("gcols", [128, 512], F32, kind="ExternalInput").ap()
    wsT_d = nc.dram_tensor("wsT", [128, GH * 128], F32, kind="ExternalInput").ap()
    bsp_d = nc.dram_tensor("bsp", [GH, 128], F32, kind="ExternalInput").ap()
    ropec = nc.dram_tensor("ropec", [64, 4], F32, kind="ExternalInput").ap()
    ident_d = nc.dram_tensor("ident", [128, 128], F32, kind="ExternalInput").ap()
    outT = nc.dram_tensor("outT", [C.D, OWN], F32, kind="ExternalOutput").ap()
    kT_scr = nc.dram_tensor("kT_scr", [AH, 128, SEQ], BF16, kind=scr_kind).ap()
    v_scr = nc.dram_tensor("v_scr", [AH, 128, SEQ], BF16, kind=scr_kind).ap()
    kr_scr = nc.dram_tensor("kr_scr", [64, SEQ], BF16, kind=scr_kind).ap()
    h_scr = nc.dram_tensor("h_scr", [DC, 128, OWN], F32, kind=scr_kind).ap()
    f_scr = nc.dram_tensor("f_scr", [DC, 128, OWN], F32, kind=scr_kind).ap()
    b_scr = nc.dram_tensor("b_scr", [2, GH * 128], BF16, kind=scr_kind).ap()

    off = {}
    o = 0
    for nm, n in (("pre", DC), ("q", QC), ("kv", KC), ("vg", GH), ("vb", GH), ("an", AH), ("gn", GH),
                  ("pm", DC), ("pf", DC), ("po", DC)):
        off[nm] = o
        o += n
    assert o <= 512

    with ExitStack() as es:
        k = K(nc, es)
        pe, act, dve, pool, sp = k.pe, k.act, k.dve, k.pool, k.sp

        arena = Arena(nc, es, C.ARENA_BYTES)

        def sb(name, shape, dt, stack=None):
            return arena.view(name, shape, dt, stack if stack is not None else es)

        ring = [sb(f"ring{i}", [128, C.TCOLS], BF16) for i in range(C.NRING)]
        ring_b = [Buf(f"ring{i}") for i in range(C.NRING)]
        banks = [es.enter_context(nc.psum_tensor(f"bank{i}", [128, 512], F32)) for i in range(8)]
        bank_b = [Buf(f"bank{i}") for i in range(8)]
        gc = sb("gc", [128, 512], F32)
        gc_b = Buf("gc")
        ident = sb("ident", [128, 128], BF16)
        ones = sb("ones", [128, 128], BF16)
        ones32 = sb("ones32", [128, 128], F32)
        wsT = sb("wsT", [128, GH * 128], BF16)
        bhl = sb("bhl", [2, GH * 128], BF16)
        b16 = sb("b16", [GH, 128], F32)
        bhi16 = sb("bhi16", [GH, 128], BF16)
        blo16 = sb("blo16", [GH, 128], BF16)
        bt16 = sb("bt16", [GH, 128], F32)
        rc = sb("rc", [64, 4], F32)
        const_b = Buf("const")

        k.dma(sp, gc[:], gcols, gc_b, writes=[gc_b])
        k.dma(sp, rc[:], ropec, const_b, writes=[const_b])
        b16b = Buf("b16")
        k.dma(sp, b16[:], bsp_d, b16b, writes=[b16b])
        identb = Buf("ident")
        k.dma(pool, ident[:], ident_d, identb, writes=[identb])
        wsTb = Buf("wsT")
        k.dma(pool, wsT[:], wsT_d, wsTb, writes=[wsTb])
        onesb = Buf("ones")
        k.op(dve, lambda: nc.vector.memset(ones[:], 1.0), writes=[onesb])
        k.op(dve, lambda: nc.vector.memset(ones32[:], 1.0), writes=[onesb])
        bhib, blob, btb, bhlb = Buf("bhi"), Buf("blo"), Buf("bt"), Buf("bhl")
        k.op(dve, lambda: nc.vector.tensor_copy(out=bhi16[:], in_=b16[:]), reads=[b16b], writes=[bhib])
        k.op(dve, lambda: nc.vector.tensor_tensor(out=bt16[:], in0=b16[:], in1=bhi16[:], op=ALU.subtract),
             reads=[b16b, bhib], writes=[btb])
        k.op(dve, lambda: nc.vector.tensor_copy(out=blo16[:], in_=bt16[:]), reads=[btb], writes=[blob])
        k.dma(sp, b_scr[0:1, :].rearrange("o (g p) -> (o g) p", g=GH), bhi16[:], bhib, reads=[bhib])
        k.dma(sp, b_scr[1:2, :].rearrange("o (g p) -> (o g) p", g=GH), blo16[:], blob, reads=[blob])
        k.wait(sp, bhib.dlast)
        k.wait(sp, blob.dlast)
        k.dma(sp, bhl[:], b_scr, bhlb, writes=[bhlb])

        def gcol(nm, i, p=128):
            return gc[0:p, off[nm] + i: off[nm] + i + 1]

        wpos = {"i": 0}
        wcursor = {n: 0 for n in windex}

        def wtile(name):
            lst = windex[name]
            idx = lst[wcursor[name] % len(lst)]
            wcursor[name] += 1
            s = wpos["i"] % C.NRING
            wpos["i"] += 1
            k.dma(pool, ring[s][:], wst[idx], ring_b[s], writes=[ring_b[s]])
            return ring[s], ring_b[s]

        bstate = {"i": 0, "reserved": set()}

        def next_bank():
            while True:
                b = bstate["i"] % 8
                bstate["i"] += 1
                if b not in bstate["reserved"]:
                    return b

        def reserve(n):
            got = []
            while len(got) < n:
                b = next_bank()
                bstate["reserved"].add(b)
                got.append(b)
            return got

        def release(bs):
            for b in bs:
                bstate["reserved"].discard(b)

        def mm(out, lhsT, rhs, start, stop):
            return lambda: nc.tensor.matmul(out, lhsT=lhsT, rhs=rhs, start=start, stop=stop)

        NW = 4
        wk = [sb(f"wk{i}", [128, 512], F32) for i in range(NW)]
        wk_b = [Buf(f"wk{i}") for i in range(NW)]
        wkc = {"i": 0}

        def work():
            i = wkc["i"] % NW
            wkc["i"] += 1
            return wk[i], wk_b[i]
        NS = 4
        sqt = [sb(f"sq{i}", [128, 512], BF16) for i in range(NS)]
        sq_b = [Buf(f"sq{i}") for i in range(NS)]
        sqc = {"i": 0}

        def rep_tile(stack, name):
            t = sb("rep_" + name + f"_{wpos['i']}_{k.pe.cnt}", [128, 512], F32, stack)
            b = fresh([Buf("rep_" + name)])[0]
            return t, b

        def recip(out_ap, in_ap, rbufs, wbufs):
            if C.FAST_RECIP:
                sc, scb = work()
                k.op(dve, lambda: nc.vector.reciprocal_approx_accurate(out=out_ap, in_=in_ap, scratch=sc[:]),
                     reads=list(rbufs), writes=list(wbufs) + [scb])
            else:
                k.op(dve, lambda: nc.vector.reciprocal(out=out_ap, in_=in_ap), reads=list(rbufs), writes=list(wbufs))

        def rstd_from(bank_i, n, out_ap, out_b):
            run_deferred()
            run_deferred()
            tmp, tmp_b = work()
            k.op(act, lambda: nc.scalar.activation(out=tmp[:], in_=banks[bank_i][:], func=AF.Sqrt,
                                                   bias=EPS, scale=1.0 / n),
                 reads=[bank_b[bank_i]], writes=[tmp_b])
            recip(out_ap, tmp[:], [tmp_b], [out_b])

        deferred = []

        def defer_pe(fn):
            deferred.append(fn)

        def run_deferred():
            for _ in range(len(deferred)):
                deferred.pop(0)()

        class Stat:
            def __init__(self, n_groups):
                run_deferred()
                self.bank = reserve(1)[0]
                self.n = n_groups
                self.i = 0

            def add_sq_of(self, src_ap, src_bufs, now=False):
                if len(deferred) >= 2:
                    run_deferred()
                j = sqc["i"] % NS
                sqc["i"] += 1
                k.op(act, lambda: nc.scalar.activation(out=sqt[j][:], in_=src_ap, func=AF.Square),
                     reads=src_bufs, writes=[sq_b[j]])
                self.add(sqt[j][:], [sq_b[j]], now)

            def add(self, ap, bufs, now=False):
                b = self.bank
                st_, sp_ = self.i == 0, self.i == self.n - 1
                self.i += 1
                bl = list(bufs)

                def go():
                    k.group(pe, [mm(banks[b][:], ones[:], ap, st_, sp_)], reads=bl + [onesb], writes=[bank_b[b]])
                if now:
                    go()
                else:
                    defer_pe(go)

            def done(self):
                run_deferred()
                run_deferred()
                assert self.i == self.n
                release([self.bank])

        def linear(wname, n_oc, kc_total, rhs_fn, rhs_bufs, epilogue):
            if kc_total >= C.TB:
                pieces = ksplit(kc_total, C.TB)
                for oc in range(n_oc):
                    b = next_bank()
                    first = True
                    for (s, n) in pieces:
                        wt, wb = wtile(wname)
                        fns = []
                        for j in range(n):
                            kc = s + j
                            fns.append(mm(banks[b][:], wt[:, j * 128:(j + 1) * 128], rhs_fn(kc),
                                          first and j == 0, kc == kc_total - 1))
                        k.group(pe, fns, reads=[wb] + rhs_bufs(s, n), writes=[bank_b[b]])
                        first = False
                    run_deferred()
                    epilogue(oc, b)
            else:
                per = C.TB // kc_total
                for o0 in range(0, n_oc, per):
                    wt, wb = wtile(wname)
                    for ol in range(min(per, n_oc - o0)):
                        b = next_bank()
                        fns = []
                        for kc in range(kc_total):
                            c0 = (ol * kc_total + kc) * 128
                            fns.append(mm(banks[b][:], wt[:, c0:c0 + 128], rhs_fn(kc), kc == 0,
                                          kc == kc_total - 1))
                        k.group(pe, fns, reads=[wb] + rhs_bufs(0, kc_total), writes=[bank_b[b]])
                        run_deferred()
                        epilogue(o0 + ol, b)

        def rope_tables(tok0, cos2, sin2, cs_b, stack):
            names = (("pi", I32), ("a", F32), ("t", F32), ("i", I32), ("r", F32), ("m", F32))
            pi_t, a_t, t_t, i_t, r_t, m_t = [sb(f"rs_{n}_{tok0}_{k.dve.cnt}", [64, 512], dt, stack)
                                             for n, dt in names]
            sb_ = fresh([Buf("ropescr")])[0]
            k.dma(sp, pi_t[:], pos[0:1, tok0:tok0 + 512].partition_broadcast(64), sb_, writes=[sb_])
            V = nc.vector

            def d(fn):
                k.op(dve, fn, reads=[sb_, const_b], writes=[sb_])
            d(lambda: V.tensor_copy(out=a_t[:], in_=pi_t[:]))
            d(lambda: V.tensor_scalar(out=a_t[:], in0=a_t[:], scalar1=rc[:, 0:1], scalar2=None, op0=ALU.mult))
            for which, dst in ((0, sin2), (1, cos2)):
                shift = 0.0 if which == 0 else PI / 2
                d(lambda: V.tensor_scalar(out=t_t[:], in0=a_t[:], scalar1=shift, scalar2=1.0 / (2 * PI),
                                          op0=ALU.add, op1=ALU.mult))
                d(lambda: V.tensor_scalar(out=t_t[:], in0=t_t[:], scalar1=0.5, scalar2=None, op0=ALU.add))
                d(lambda: V.tensor_copy(out=i_t[:], in_=t_t[:]))
                d(lambda: V.tensor_copy(out=t_t[:], in_=i_t[:]))
                C1 = 6.28125
                C2 = 2 * PI - C1
                d(lambda: V.scalar_tensor_tensor(out=r_t[:], in0=t_t[:], scalar=-C1, in1=a_t[:],
                                                 op0=ALU.mult, op1=ALU.add))
                d(lambda: V.scalar_tensor_tensor(out=r_t[:], in0=t_t[:], scalar=-C2, in1=r_t[:],
                                                 op0=ALU.mult, op1=ALU.add))
                if shift:
                    d(lambda: V.tensor_scalar(out=r_t[:], in0=r_t[:], scalar1=shift, scalar2=None, op0=ALU.add))
                d(lambda: V.tensor_scalar(out=m_t[:], in0=r_t[:], scalar1=-PI, scalar2=2 * PI,
                                          op0=ALU.is_lt, op1=ALU.mult))
                d(lambda: V.tensor_tensor(out=r_t[:], in0=r_t[:], in1=m_t[:], op=ALU.add))
                d(lambda: V.tensor_scalar(out=m_t[:], in0=r_t[:], scalar1=PI, scalar2=-2 * PI,
                                          op0=ALU.is_gt, op1=ALU.mult))
                d(lambda: V.tensor_tensor(out=r_t[:], in0=r_t[:], in1=m_t[:], op=ALU.add))
                d(lambda: V.tensor_scalar(out=r_t[:], in0=r_t[:], scalar1=3.1415925, scalar2=-3.1415925,
                                          op0=ALU.min, op1=ALU.max))
                k.op(act, lambda: nc.scalar.activation(out=dst, in_=r_t[:], func=AF.Sin), reads=[sb_],
                     writes=[cs_b])
            k.op(dve, lambda: V.tensor_scalar(out=sin2, in0=sin2, scalar1=rc[:, 1:2], scalar2=None, op0=ALU.mult),
                 reads=[cs_b, const_b], writes=[cs_b])
            return [sb_]

        def apply_rope(bankA, bankB, cos2, sin2, cs_b, scale_rep, out_ap, out_bufs):
            t1, t1b = work()
            t2, t2b = work()
            V = nc.vector
            k.op(dve, lambda: V.tensor_tensor(out=t1[0:64, :], in0=banks[bankA][0:64, :], in1=cos2[:], op=ALU.mult),
                 reads=[bank_b[bankA], cs_b], writes=[t1b])
            k.op(dve, lambda: V.tensor_tensor(out=t2[0:64, :], in0=banks[bankB][0:64, :], in1=sin2[:], op=ALU.mult),
                 reads=[bank_b[bankB], cs_b], writes=[t2b])
            if scale_rep is None:
                k.op(dve, lambda: V.tensor_tensor(out=out_ap, in0=t1[0:64, :], in1=t2[0:64, :], op=ALU.add),
                     reads=[t1b, t2b], writes=out_bufs)
            else:
                sr, srb = scale_rep
                k.op(dve, lambda: V.tensor_tensor(out=t1[0:64, :], in0=t1[0:64, :], in1=t2[0:64, :], op=ALU.add),
                     reads=[t1b, t2b], writes=[t1b])
                k.op(dve, lambda: V.tensor_tensor(out=out_ap, in0=t1[0:64, :], in1=sr[0:64, :], op=ALU.mult),
                     reads=[t1b, srb], writes=out_bufs)

        XG = 4

        def make_xg_gen(tok0, xg, xg_b, xbuf, xbuf_b, rx, rxb, now=True):
            XGl = xbuf[0].shape[1]

            def load_x(gi):
                j = gi % len(xbuf)
                src = xT[gi * XGl * 128:(gi + 1) * XGl * 128, tok0:tok0 + 512].rearrange("(g p) t -> p g t", p=128)
                k.dma(sp, xbuf[j][:], src, xbuf_b[j], writes=[xbuf_b[j]])
                return xbuf[j], xbuf_b[j]
            st = Stat(DC)
            ng = DC // XGl
            nb_ = len(xbuf)
            dist = nb_ - 1
            loaded = {}
            for gi in range(min(dist, ng)):
                loaded[gi] = load_x(gi)
            for gi in range(ng):
                if gi + dist < ng:
                    loaded[gi + dist] = load_x(gi + dist)
                xb_, xbb = loaded.pop(gi)
                for g in range(XGl):
                    fc = gi * XGl + g
                    st.add_sq_of(xb_[:, g, :], [xbb], now=now)
                    k.op(dve, lambda: nc.vector.tensor_scalar(out=xg[:, fc * 512:(fc + 1) * 512], in0=xb_[:, g, :],
                                                              scalar1=gcol("pre", fc), scalar2=None, op0=ALU.mult),
                         reads=[xbb, gc_b], writes=[xg_b[fc]])
                    yield
            rstd_from(st.bank, C.D, rx[:], rxb)
            st.done()

        def make_xg(*a):
            for _ in make_xg_gen(*a):
                pass

        with ExitStack() as pa:
            xg = sb("A_xg", [128, DC * 512], BF16, pa)
            xg_b = [Buf(f"A_xg{i}") for i in range(DC)]
            xbuf = [sb(f"A_xbuf{i}", [128, XG, 512], F32, pa) for i in range(3)]
            xbuf_b = [Buf(f"A_xbuf{i}") for i in range(3)]
            cosA = [sb(f"A_cos2_{i}", [64, 512], F32, pa) for i in range(2)]
            sinA = [sb(f"A_sin2_{i}", [64, 512], F32, pa) for i in range(2)]
            csA_b = [Buf(f"A_cossin{i}") for i in range(2)]
            nuk = len(windex["ukvk"])
            nuv = len(windex["ukvv"])
            ukw = [sb(f"A_ukw{i}", [128, C.TCOLS], BF16, pa) for i in range(nuk + nuv)]
            ukw_b = [Buf(f"A_ukw{i}") for i in range(nuk + nuv)]
            for i in range(nuk):
                k.dma(pool, ukw[i][:], wst[windex["ukvk"][i]], ukw_b[i], writes=[ukw_b[i]])
            for i in range(nuv):
                k.dma(pool, ukw[nuk + i][:], wst[windex["ukvv"][i]], ukw_b[nuk + i], writes=[ukw_b[nuk + i]])
            kvc = sb("A_kvc", [128, KC * 512], F32, pa)
            kvc_b = [Buf(f"A_kvc{i}") for i in range(KC)]
            kvn = sb("A_kvn", [128, KC * 512], BF16, pa)
            kvn_b = [Buf(f"A_kvn{i}") for i in range(KC)]
            kst = [sb(f"A_kst{i}", [128, 4 * 512], BF16, pa) for i in range(2)]
            kst_b = [Buf(f"A_kst{i}") for i in range(2)]
            vst = [sb(f"A_vst{i}", [128, 4 * 512], BF16, pa) for i in range(2)]
            vst_b = [Buf(f"A_vst{i}") for i in range(2)]
            krs = sb("A_krs", [64, 512], BF16, pa)
            krs_b = Buf("A_krs")
            rx, rxb = rep_tile(pa, "rx")
            r2, r2b = rep_tile(pa, "r2")
            ropeb = []
            scr_ev = []

            def ropeprep(blk_):
                with ExitStack() as prs:
                    rb_ = rope_tables(blk_ * 512, cosA[blk_ % 2][:], sinA[blk_ % 2][:], csA_b[blk_ % 2], prs)
                retire(rb_)

            def prep(blk_):
                return make_xg_gen(blk_ * 512, xg, xg_b, xbuf, xbuf_b, rx, rxb, now=(blk_ == 0 or not C.DEFER_A))
            ropeprep(0)
            for _ in prep(0):
                pass
            for blk in range(C.NBLK):
                tok0 = blk * 512
                cos2, sin2, cs_b = cosA[blk % 2], sinA[blk % 2], csA_b[blk % 2]
                if blk + 1 < C.NBLK:
                    ropeprep(blk + 1)

                def ep_lat(oc, b):
                    k.op(dve, lambda: nc.vector.tensor_tensor(out=kvc[:, oc * 512:(oc + 1) * 512], in0=banks[b][:],
                                                              in1=rx[:], op=ALU.mult),
                         reads=[bank_b[b], rxb], writes=[kvc_b[oc]])
                linear("kv", KC, DC, lambda kc: xg[:, kc * 512:(kc + 1) * 512], lambda s, n: xg_b[s:s + n], ep_lat)
                wt, wb = wtile("kv")
                bA, bB = next_bank(), next_bank()
                for (bb, cofs) in ((bA, 0), (bB, 64)):
                    fns = []
                    for kc in range(DC):
                        fns.append(mm(banks[bb][0:64, :], wt[:, kc * 128 + cofs:kc * 128 + cofs + 64],
                                      xg[:, kc * 512:(kc + 1) * 512], kc == 0, kc == DC - 1))
                    k.group(pe, fns, reads=[wb] + xg_b, writes=[bank_b[bb]])
                apply_rope(bA, bB, cos2, sin2, cs_b, (rx, rxb), krs[:], [krs_b])
                k.dma(sp, kr_scr[:, tok0:tok0 + 512], krs[:], krs_b, reads=[krs_b])
                st = Stat(KC)
                for oc in range(KC):
                    st.add_sq_of(kvc[:, oc * 512:(oc + 1) * 512], [kvc_b[oc]])
                rstd_from(st.bank, C.KVL, r2[:], r2b)
                st.done()
                for oc in range(KC):
                    k.op(dve, lambda: nc.vector.scalar_tensor_tensor(
                        out=kvn[:, oc * 512:(oc + 1) * 512], in0=kvc[:, oc * 512:(oc + 1) * 512],
                        scalar=gcol("kv", oc), in1=r2[:], op0=ALU.mult, op1=ALU.mult),
                        reads=[kvc_b[oc], r2b, gc_b], writes=[kvn_b[oc]])
                gen = prep(blk + 1) if blk + 1 < C.NBLK else iter(())

                def step():
                    next(gen, None)
                per = max(1, C.TB // KC)
                for hg in range(AH // 4):
                    sj = hg % 2
                    for hl in range(4):
                        h = hg * 4 + hl
                        b = next_bank()
                        t_, b_ = ukw[h // per], ukw_b[h // per]
                        fns = []
                        for kc in range(KC):
                            c0 = ((h % per) * KC + kc) * 128
                            fns.append(mm(banks[b][:], t_[:, c0:c0 + 128], kvn[:, kc * 512:(kc + 1) * 512], kc == 0,
                                          kc == KC - 1))
                        k.group(pe, fns, reads=[b_] + kvn_b, writes=[bank_b[b]])
                        run_deferred()
                        k.op(act, lambda: nc.scalar.copy(out=kst[sj][:, hl * 512:(hl + 1) * 512], in_=banks[b][:]),
                             reads=[bank_b[b]], writes=[kst_b[sj]])
                        step()
                    kdst = kT_scr[hg * 4:(hg + 1) * 4, :, tok0:tok0 + 512].rearrange("h p t -> p h t")
                    k.dma(sp, kdst, kst[sj][:].rearrange("p (h t) -> p h t", h=4), kst_b[sj], reads=[kst_b[sj]])
                vcols = AH * 128
                for hg in range(AH // 4):
                    sj = hg % 2
                    vv = vst[sj][:].rearrange("p (h t d) -> p h t d", h=4, t=4)
                    for tt in range(4):
                        b = next_bank()
                        fns, rd = [], []
                        for kc in range(KC):
                            colg = kc * vcols + hg * 512
                            ti, c0 = colg // C.TCOLS, colg % C.TCOLS
                            if ukw_b[nuk + ti] not in rd:
                                rd.append(ukw_b[nuk + ti])
                            fns.append(mm(banks[b][:], kvn[:, kc * 512 + tt * 128: kc * 512 + (tt + 1) * 128],
                                          ukw[nuk + ti][:, c0:c0 + 512], kc == 0, kc == KC - 1))
                        k.group(pe, fns, reads=rd + kvn_b, writes=[bank_b[b]])
                        run_deferred()
                        src = banks[b][:].rearrange("p (h d) -> p h d", h=4)
                        dstv = vv[:, :, tt, :]
                        if tt % 2 == 0:
                            k.op(act, lambda: nc.scalar.copy(out=dstv, in_=src), reads=[bank_b[b]],
                                 writes=[vst_b[sj]])
                        else:
                            k.op(dve, lambda: nc.vector.tensor_copy(out=dstv, in_=src), reads=[bank_b[b]],
                                 writes=[vst_b[sj]])
                        step()
                    vdst = v_scr[hg * 4:(hg + 1) * 4, :, tok0:tok0 + 512].rearrange("h p t -> p h t")
                    k.dma(sp, vdst, vst[sj][:].rearrange("p (h t) -> p h t", h=4), vst_b[sj], reads=[vst_b[sj]])
                for _ in gen:
                    pass
            retire(xg_b + xbuf_b + csA_b + ukw_b + kvc_b + kvn_b + kst_b + vst_b + [krs_b, rxb, r2b])
        scr_guard = fresh([Buf("scr_guard")])[0]

        scale = 1.0 / math.sqrt(192.0)
        for half in range(C.NH):
            tok0 = half * T
            with ExitStack() as ph:
                P_ = f"h{half}_"
                gnT = sb(P_ + "gnT", [128, GH * 512], BF16, ph)
                gn_b = fresh([Buf(f"gn{i}") for i in range(GH)])
                anT = sb(P_ + "anT", [128, AH * 512], BF16, ph)
                an_b = fresh([Buf(f"an{i}") for i in range(AH)])
                ra, rab = rep_tile(ph, "ra")
                pqn = ExitStack()
                qn = sb(P_ + "qn", [128, QC * 512], BF16, pqn)
                qn_b = fresh([Buf(f"qn{i}") for i in range(QC)])
                pcs = ExitStack()
                cos2 = sb(P_ + "cos2", [64, 512], F32, pcs)
                sin2 = sb(P_ + "sin2", [64, 512], F32, pcs)
                cs_b = fresh([Buf("cossin")])[0]
                with ExitStack() as prs:
                    rb_ = rope_tables(tok0, cos2[:], sin2[:], cs_b, prs)
                retire(rb_)

                with ExitStack() as pb:
                    xg = sb(P_ + "xg", [128, DC * 512], BF16, pb)
                    xg_b = fresh([Buf(f"B_xg{i}") for i in range(DC)])
                    xbuf = [sb(P_ + f"xbuf{i}", [128, 2, 512], F32, pb) for i in range(3)]
                    xbuf_b = fresh([Buf(f"xbuf{i}") for i in range(3)])
                    rx, rxb = rep_tile(pb, "rx")
                    make_xg(tok0, xg, xg_b, xbuf, xbuf_b, rx, rxb)

                    def rhs_x(kc):
                        return xg[:, kc * 512:(kc + 1) * 512]

                    def rhsb_x(s, n):
                        return xg_b[s:s + n]

                    with ExitStack() as pq:
                        qc = sb(P_ + "qc", [128, QC * 512], F32, pq)
                        qc_b = fresh([Buf(f"qc{i}") for i in range(QC)])
                        rq, rqb = rep_tile(pq, "rq")
                        st = Stat(QC)

                        def ep_q(oc, b):
                            k.op(dve, lambda: nc.vector.tensor_tensor(out=qc[:, oc * 512:(oc + 1) * 512],
                                                                      in0=banks[b][:], in1=rx[:], op=ALU.mult),
                                 reads=[bank_b[b], rxb], writes=[qc_b[oc]])
                            st.add_sq_of(qc[:, oc * 512:(oc + 1) * 512], [qc_b[oc]])
                        linear("q", QC, DC, rhs_x, rhsb_x, ep_q)
                        rstd_from(st.bank, C.QL, rq[:], rqb)
                        st.done()
                        for oc in range(QC):
                            k.op(dve, lambda: nc.vector.scalar_tensor_tensor(
                                out=qn[:, oc * 512:(oc + 1) * 512], in0=qc[:, oc * 512:(oc + 1) * 512],
                                scalar=gcol("q", oc), in1=rq[:], op0=ALU.mult, op1=ALU.mult),
                                reads=[qc_b[oc], rqb, gc_b], writes=[qn_b[oc]])
                        retire(qc_b + [rqb])

                    vt = sb(P_ + "vt", [128, GH * 512], F32, pb)
                    vt_b = fresh([Buf(f"vt{i}") for i in range(GH)])
                    vb16 = [sb(P_ + f"vb16_{i}", [128, 512], BF16, pb) for i in range(2)]
                    vb16_b = fresh([Buf(f"vb16_{i}") for i in range(2)])
                    mean, meanb = rep_tile(pb, "mean")
                    rv, rvb = rep_tile(pb, "rv")
                    s1 = reserve(1)[0]
                    st2 = Stat(GH)
                    vcnt = {"i": 0}

                    def ep_v(oc, b):
                        t, tb = work()
                        k.op(dve, lambda: nc.vector.tensor_tensor(out=t[:], in0=banks[b][:], in1=rx[:], op=ALU.mult),
                             reads=[bank_b[b], rxb], writes=[tb])
                        k.op(act, lambda: nc.scalar.activation(out=vt[:, oc * 512:(oc + 1) * 512], in_=t[:],
                                                               func=AF.Gelu_apprx_tanh), reads=[tb], writes=[vt_b[oc]])
                        j = vcnt["i"] % 2
                        vcnt["i"] += 1
                        k.op(dve, lambda: nc.vector.tensor_copy(out=vb16[j][:], in_=vt[:, oc * 512:(oc + 1) * 512]),
                             reads=[vt_b[oc]], writes=[vb16_b[j]])
                        defer_pe(lambda j=j, oc=oc: k.group(
                            pe, [mm(banks[s1][:], ones[:], vb16[j][:], oc == 0, oc == GH - 1)],
                            reads=[vb16_b[j], onesb], writes=[bank_b[s1]]))
                        st2.add_sq_of(vt[:, oc * 512:(oc + 1) * 512], [vt_b[oc]])
                    linear("v", GH, DC, rhs_x, rhsb_x, ep_v)
                    run_deferred()
                    t, tb = work()
                    t2, t2b = work()
                    k.op(dve, lambda: nc.vector.tensor_scalar(out=mean[:], in0=banks[s1][:], scalar1=1.0 / C.GW,
                                                              scalar2=None, op0=ALU.mult),
                         reads=[bank_b[s1]], writes=[meanb])
                    k.op(dve, lambda: nc.vector.tensor_tensor(out=t[:], in0=mean[:], in1=mean[:], op=ALU.mult),
                         reads=[meanb], writes=[tb])
                    k.op(dve, lambda: nc.vector.scalar_tensor_tensor(out=t2[:], in0=banks[st2.bank][:],
                                                                     scalar=1.0 / C.GW, in1=t[:], op0=ALU.mult,
                                                                     op1=ALU.subtract),
                         reads=[bank_b[st2.bank], tb], writes=[t2b])
                    k.op(dve, lambda: nc.vector.tensor_scalar(out=t2[:], in0=t2[:], scalar1=0.0, scalar2=None,
                                                              op0=ALU.max), reads=[t2b], writes=[t2b])
                    k.op(act, lambda: nc.scalar.activation(out=t[:], in_=t2[:], func=AF.Sqrt, bias=EPS, scale=1.0),
                         reads=[t2b], writes=[tb])
                    recip(rv[:], t[:], [tb], [rvb])
                    release([s1])
                    st2.done()
                    vln = sb(P_ + "vln", [128, 4 * C.GW], BF16, pb)
                    vln_b = fresh([Buf(f"vln{i}") for i in range(GH)])
                    vln_v = vln[:].rearrange("p (t c) -> p t c", t=4)

                    def ln_apply(c):
                        t, tb = work()
                        k.op(dve, lambda: nc.vector.tensor_tensor(out=t[:], in0=vt[:, c * 512:(c + 1) * 512],
                                                                  in1=mean[:], op=ALU.subtract),
                             reads=[vt_b[c], meanb], writes=[tb])
                        k.op(dve, lambda: nc.vector.tensor_tensor(out=t[:], in0=t[:], in1=rv[:], op=ALU.mult),
                             reads=[tb, rvb], writes=[tb])
                        j = c % 2
                        k.op(act, lambda: nc.scalar.activation(out=vb16[j][:], in_=t[:], func=AF.Identity,
                                                               bias=gcol("vb", c), scale=gcol("vg", c)),
                             reads=[tb, gc_b], writes=[vb16_b[j]])

                        def go():
                            b = next_bank()
                            pb16 = banks[b][:].bitcast(BF16)
                            fns = [(lambda tt=tt: nc.tensor.transpose(out=pb16[:, tt * 128:(tt + 1) * 128],
                                                                      in_=vb16[j][:, tt * 128:(tt + 1) * 128],
                                                                      identity=ident[:])) for tt in range(4)]
                            k.group(pe, fns, reads=[vb16_b[j], identb], writes=[bank_b[b]])
                            k.op(act, lambda: nc.scalar.copy(out=vln_v[:, :, c * 128:(c + 1) * 128],
                                                             in_=pb16[:, 0:512].rearrange("p (t c) -> p t c", t=4)),
                                 reads=[bank_b[b]], writes=[vln_b[c]])
                        defer_pe(go)

                    def ep_u(oc, b):
                        t, tb = work()
                        k.op(dve, lambda: nc.vector.tensor_tensor(out=t[:], in0=banks[b][:], in1=rx[:], op=ALU.mult),
                             reads=[bank_b[b], rxb], writes=[tb])
                        k.op(act, lambda: nc.scalar.activation(out=gnT[:, oc * 512:(oc + 1) * 512], in_=t[:],
                                                               func=AF.Gelu_apprx_tanh), reads=[tb], writes=[gn_b[oc]])
                        ln_apply(oc)
                    linear("u", GH, DC, rhs_x, rhsb_x, ep_u)
                    run_deferred()

                    rg, rgb = rep_tile(pb, "rg")
                    st = Stat(GH)
                    for g in range(GH):
                        b = next_bank()
                        fns = []
                        for tt in range(4):
                            fns.append(mm(banks[b][:, tt * 128:(tt + 1) * 128],
                                          vln_v[:, tt, g * 128:(g + 1) * 128], wsT[:, g * 128:(g + 1) * 128],
                                          True, False))
                            fns.append(mm(banks[b][:, tt * 128:(tt + 1) * 128], ones[0:2, :],
                                          bhl[0:2, g * 128:(g + 1) * 128], False, True))
                        k.group(pe, fns, reads=[vln_b[g], wsTb, bhlb, onesb], writes=[bank_b[b]])
                        k.op(dve, lambda: nc.vector.tensor_tensor(out=vt[:, g * 512:(g + 1) * 512], in0=banks[b][:],
                                                                  in1=gnT[:, g * 512:(g + 1) * 512], op=ALU.mult),
                             reads=[bank_b[b], gn_b[g]], writes=[vt_b[g]])
                        st.add_sq_of(vt[:, g * 512:(g + 1) * 512], [vt_b[g]])
                    rstd_from(st.bank, C.GW, rg[:], rgb)
                    st.done()
                    for g in range(GH):
                        k.op(dve, lambda: nc.vector.scalar_tensor_tensor(
                            out=gnT[:, g * 512:(g + 1) * 512], in0=vt[:, g * 512:(g + 1) * 512],
                            scalar=gcol("gn", g), in1=rg[:], op0=ALU.mult, op1=ALU.mult),
                            reads=[vt_b[g], rgb, gc_b], writes=[gn_b[g]])
                    retire(xg_b + xbuf_b + vt_b + vb16_b + vln_b + [rxb, meanb, rvb, rgb])

                pqk = ExitStack()
                QnT = sb(P_ + "QnT", [128, AH * 512], BF16, pqk)
                QnT_b = fresh([Buf(f"QnT{i}") for i in range(AH)])
                QrT = sb(P_ + "QrT", [128, AH * 512], BF16, pqk)
                QrT_b = fresh([Buf(f"QrT{i}") for i in range(AH)])
                k.op(dve, lambda: nc.vector.memset(QrT[64:128, :], 0.0), writes=QrT_b)
                perq = C.TB // QC
                assert perq % 2 == 0 and perq >= 2
                for h0 in range(0, AH, perq // 2):
                    wt, wb = wtile("uq")
                    for hl in range(min(perq // 2, AH - h0)):
                        h = h0 + hl
                        bn, bA, bB = next_bank(), next_bank(), next_bank()
                        for (bb, colofs, M) in ((bn, 0, 128), (bA, 128, 64), (bB, 192, 64)):
                            fns = []
                            for kc in range(QC):
                                c0 = ((hl * 2 + colofs // 128) * QC + kc) * 128 + (colofs % 128)
                                fns.append(mm(banks[bb][0:M, :], wt[:, c0:c0 + M], qn[:, kc * 512:(kc + 1) * 512],
                                              kc == 0, kc == QC - 1))
                            k.group(pe, fns, reads=[wb] + qn_b, writes=[bank_b[bb]])
                        k.op(act, lambda: nc.scalar.copy(out=QnT[:, h * 512:(h + 1) * 512], in_=banks[bn][:]),
                             reads=[bank_b[bn]], writes=[QnT_b[h]])
                        apply_rope(bA, bB, cos2, sin2, cs_b, None, QrT[0:64, h * 512:(h + 1) * 512], [QrT_b[h]])
                retire([cs_b] + qn_b)
                pcs.close()
                pqn.close()

                with ExitStack() as pd:
                    Kb = [sb(P_ + f"K{i}", [128, SEQ], BF16, pd) for i in range(2)]
                    Kb_b = fresh([Buf(f"K{i}") for i in range(2)])
                    Vb = [sb(P_ + f"V{i}", [128, SEQ], BF16, pd) for i in range(2)]
                    Vb_b = fresh([Buf(f"V{i}") for i in range(2)])
                    krT = sb(P_ + "krT", [128, SEQ], BF16, pd)
                    krT_b = fresh([Buf("krT")])[0]
                    k.op(dve, lambda: nc.vector.memset(krT[64:128, :], 0.0), writes=[krT_b])
                    NP = 4
                    PT = [sb(P_ + f"PT{i}", [128, 512], BF16, pd) for i in range(NP)]
                    PT_b = fresh([Buf(f"PT{i}") for i in range(NP)])
                    rz = [sb(P_ + f"rz{i}", [128, 512], F32, pd) for i in range(2)]
                    rz_b = fresh([Buf(f"rz{i}") for i in range(2)])
                    k.dma(sp, krT[0:64, :], kr_scr, krT_b, reads=[scr_guard], writes=[krT_b])

                    def load_kv(h):
                        j = h % 2
                        k.dma(sp, Kb[j][:], kT_scr[h], Kb_b[j], reads=[scr_guard], writes=[Kb_b[j]])
                        k.dma(sp, Vb[j][:], v_scr[h], Vb_b[j], reads=[scr_guard], writes=[Vb_b[j]])
                    load_kv(0)
                    zacc = [sb(P_ + f"zacc{i}", [128, 512], F32, pd) for i in range(4)]
                    zacc_b = fresh([Buf(f"zacc{i}") for i in range(4)])
                    accS = sb(P_ + "accS", [128, 512], F32, pd)
                    accS_b = fresh([Buf("accS")])[0]
                    run_deferred()
                    sbanks = reserve(3)
                    oz = reserve(4)
                    ptc = {"i": 0}
                    NKT = C.NKT
                    for h in range(AH):
                        j = h % 2
                        if h + 1 < AH:
                            load_kv(h + 1)
                        bO, bZ = oz[2 * j], oz[2 * j + 1]
                        zA, zAb, zB, zBb = zacc[2 * j], zacc_b[2 * j], zacc[2 * j + 1], zacc_b[2 * j + 1]

                        def emit_S(kb):
                            b = sbanks[kb % 3]
                            fns = [mm(banks[b][:], Kb[j][:, kb * 128:(kb + 1) * 128], QnT[:, h * 512:(h + 1) * 512],
                                      True, False),
                                   mm(banks[b][:], krT[:, kb * 128:(kb + 1) * 128], QrT[:, h * 512:(h + 1) * 512],
                                      False, True)]
                            k.group(pe, fns, reads=[Kb_b[j], QnT_b[h], QrT_b[h], krT_b], writes=[bank_b[b]])
                            p = ptc["i"] % NP
                            ptc["i"] += 1
                            k.op(act, lambda: nc.scalar.activation(out=PT[p][:], in_=banks[b][:], func=AF.Exp,
                                                                   scale=scale), reads=[bank_b[b]], writes=[PT_b[p]])
                            return p
                        pend = [emit_S(kb) for kb in range(min(2, NKT))]
                        run_deferred()
                        for kb in range(NKT):
                            if kb + 2 < NKT:
                                pend.append(emit_S(kb + 2))
                            p = pend.pop(0)
                            if kb == 3:
                                run_deferred()
                            k.group(pe, [mm(banks[bO][:], Vb[j][:, kb * 128:(kb + 1) * 128], PT[p][:], kb == 0,
                                            kb == NKT - 1)], reads=[Vb_b[j], PT_b[p]], writes=[bank_b[bO]])
                            za, zab = (zA, zAb) if kb % 2 == 0 else (zB, zBb)
                            ze, zr = (dve, nc.vector) if (kb % 2 == 0 or not C.POOL_Z) else (pool, nc.gpsimd)
                            if kb < 2:
                                k.op(ze, lambda: zr.tensor_copy(out=za[:], in_=PT[p][:]), reads=[PT_b[p]],
                                     writes=[zab])
                            else:
                                k.op(ze, lambda: zr.tensor_tensor(out=za[:], in0=za[:], in1=PT[p][:], op=ALU.add),
                                     reads=[zab, PT_b[p]], writes=[zab])

                        def head_end(h=h, j=j, bO=bO, bZ=bZ, zA=zA, zAb=zAb, zB=zB, zBb=zBb):
                            if NKT > 1:
                                k.op(dve, lambda: nc.vector.tensor_tensor(out=zA[:], in0=zA[:], in1=zB[:], op=ALU.add),
                                     reads=[zAb, zBb], writes=[zAb])
                            k.group(pe, [mm(banks[bZ][:], ones32[:], zA[:], True, True)], reads=[zAb, onesb],
                                    writes=[bank_b[bZ]])
                            recip(rz[j][:], banks[bZ][:], [bank_b[bZ]], [rz_b[j]])
                            t, tb = work()
                            k.op(dve, lambda: nc.vector.tensor_tensor(out=t[:], in0=banks[bO][:], in1=rz[j][:],
                                                                      op=ALU.mult),
                                 reads=[bank_b[bO], rz_b[j]], writes=[tb])
                            if h == 0:
                                k.op(act, lambda: nc.scalar.activation(out=accS[:], in_=t[:], func=AF.Square),
                                     reads=[tb], writes=[accS_b])
                            else:
                                s32, s32b = work()
                                k.op(act, lambda: nc.scalar.activation(out=s32[:], in_=t[:], func=AF.Square),
                                     reads=[tb], writes=[s32b])
                                k.op(dve, lambda: nc.vector.tensor_tensor(out=accS[:], in0=accS[:], in1=s32[:],
                                                                          op=ALU.add),
                                     reads=[accS_b, s32b], writes=[accS_b])
                            k.op(dve, lambda: nc.vector.tensor_scalar(out=anT[:, h * 512:(h + 1) * 512], in0=t[:],
                                                                      scalar1=gcol("an", h), scalar2=None,
                                                                      op0=ALU.mult),
                                 reads=[tb, gc_b], writes=[an_b[h]])
                        defer_pe(head_end)
                    run_deferred()
                    release(sbanks)
                    release(oz)
                    bS = next_bank()
                    k.group(pe, [mm(banks[bS][:], ones32[:], accS[:], True, True)], reads=[accS_b, onesb],
                            writes=[bank_b[bS]])
                    rstd_from(bS, C.AW, ra[:], rab)
                    retire(Kb_b + Vb_b + [krT_b] + PT_b + rz_b + QnT_b + QrT_b + zacc_b + [accS_b])
                pqk.close()

                with ExitStack() as pe_:
                    mo = sb(P_ + "mo", [128, DC * 512], F32, pe_)
                    mo_b = fresh([Buf(f"mo{i}") for i in range(DC)])
                    xbuf = [sb(P_ + f"Exbuf{i}", [128, XG, 512], F32, pe_) for i in range(2)]
                    xbuf_b = fresh([Buf(f"Exbuf{i}") for i in range(2)])
                    rp, rpb = rep_tile(pe_, "rp")
                    rh, rhb = rep_tile(pe_, "rh")
                    st = Stat(DC)
                    for oc in range(DC):
                        wt, wb = wtile("out")
                        bA, bG = next_bank(), next_bank()
                        fns = [mm(banks[bA][:], wt[:, kc * 128:(kc + 1) * 128], anT[:, kc * 512:(kc + 1) * 512],
                                  kc == 0, kc == AH - 1) for kc in range(AH)]
                        k.group(pe, fns, reads=[wb] + an_b, writes=[bank_b[bA]])
                        fns = [mm(banks[bG][:], wt[:, (AH + kc) * 128:(AH + kc + 1) * 128],
                                  gnT[:, kc * 512:(kc + 1) * 512], kc == 0, kc == GH - 1) for kc in range(GH)]
                        k.group(pe, fns, reads=[wb] + gn_b, writes=[bank_b[bG]])
                        t, tb = work()
                        k.op(dve, lambda: nc.vector.tensor_tensor(out=t[:], in0=banks[bA][:], in1=ra[:], op=ALU.mult),
                             reads=[bank_b[bA], rab], writes=[tb])
                        sl = slice(oc * 512, (oc + 1) * 512)
                        k.op(dve, lambda: nc.vector.tensor_tensor(out=mo[:, sl], in0=t[:], in1=banks[bG][:],
                                                                  op=ALU.add),
                             reads=[tb, bank_b[bG]], writes=[mo_b[oc]])
                        st.add_sq_of(mo[:, sl], [mo_b[oc]])
                    rstd_from(st.bank, C.D, rp[:], rpb)
                    st.done()
                    st = Stat(DC)

                    def load_x(gi):
                        j = gi % 2
                        src = xT[gi * XG * 128:(gi + 1) * XG * 128, tok0:tok0 + 512].rearrange(
                            "(g p) t -> p g t", p=128)
                        k.dma(sp, xbuf[j][:], src, xbuf_b[j], writes=[xbuf_b[j]])
                        return xbuf[j], xbuf_b[j]
                    ng = DC // XG
                    nxt = load_x(0)
                    for gi in range(ng):
                        xb_, xbb = nxt
                        if gi + 1 < ng:
                            nxt = load_x(gi + 1)
                        for g in range(XG):
                            d = gi * XG + g
                            sl = slice(d * 512, (d + 1) * 512)
                            k.op(dve, lambda: nc.vector.scalar_tensor_tensor(
                                out=mo[:, sl], in0=mo[:, sl], scalar=gcol("pm", d), in1=rp[:], op0=ALU.mult,
                                op1=ALU.mult), reads=[mo_b[d], rpb, gc_b], writes=[mo_b[d]])
                            k.op(dve, lambda: nc.vector.tensor_tensor(out=mo[:, sl], in0=mo[:, sl], in1=xb_[:, g, :],
                                                                      op=ALU.add),
                                 reads=[mo_b[d], xbb], writes=[mo_b[d]])
                            k.dma(sp, h_scr[d][:, tok0:tok0 + T], mo[:, sl], mo_b[d], reads=[mo_b[d]])
                            st.add_sq_of(mo[:, sl], [mo_b[d]])
                    rstd_from(st.bank, C.D, rh[:], rhb)
                    st.done()
                    hn_b = an_b + gn_b

                    def hn_ap(d):
                        if d < AH:
                            return anT[:, d * 512:(d + 1) * 512]
                        return gnT[:, (d - AH) * 512:(d - AH + 1) * 512]
                    for d in range(DC):
                        sl = slice(d * 512, (d + 1) * 512)
                        k.op(dve, lambda: nc.vector.scalar_tensor_tensor(
                            out=hn_ap(d), in0=mo[:, sl], scalar=gcol("pf", d), in1=rh[:], op0=ALU.mult,
                            op1=ALU.mult), reads=[mo_b[d], rhb, gc_b], writes=[hn_b[d]])
                    retire(mo_b + xbuf_b + [rpb, rhb, rab])

                rf, rfb = rep_tile(ph, "rf")
                with ExitStack() as pf:
                    actT = sb(P_ + "act", [128, FC * 512], BF16, pf)
                    act_b = fresh([Buf(f"act{i}") for i in range(FC)])
                    fstage = [sb(P_ + f"fs{i}", [128, 512], F32, pf) for i in range(3)]
                    fstage_b = fresh([Buf(f"fs{i}") for i in range(3)])
                    for fc in range(FC):
                        bg, bu = next_bank(), next_bank()
                        for bb in (bg, bu):
                            wt, wb = wtile("gu")
                            fns = [mm(banks[bb][:], wt[:, kc * 128:(kc + 1) * 128], hn_ap(kc), kc == 0, kc == DC - 1)
                                   for kc in range(DC)]
                            k.group(pe, fns, reads=[wb] + hn_b, writes=[bank_b[bb]])
                        t, tb = work()
                        k.op(act, lambda: nc.scalar.activation(out=t[:], in_=banks[bg][:], func=AF.Silu),
                             reads=[bank_b[bg]], writes=[tb])
                        k.op(dve, lambda: nc.vector.tensor_tensor(out=actT[:, fc * 512:(fc + 1) * 512], in0=t[:],
                                                                  in1=banks[bu][:], op=ALU.mult),
                             reads=[tb, bank_b[bu]], writes=[act_b[fc]])
                    st = Stat(DC)

                    def ep_d(oc, b):
                        j = oc % 3
                        k.op(act, lambda: nc.scalar.copy(out=fstage[j][:], in_=banks[b][:]), reads=[bank_b[b]],
                             writes=[fstage_b[j]])
                        st.add_sq_of(fstage[j][:], [fstage_b[j]])
                        k.dma(sp, f_scr[oc][:, tok0:tok0 + T], fstage[j][:], fstage_b[j], reads=[fstage_b[j]])
                    linear("down", DC, FC, lambda kc: actT[:, kc * 512:(kc + 1) * 512],
                           lambda s, n: act_b[s:s + n], ep_d)
                    rstd_from(st.bank, C.D, rf[:], rfb)
                    st.done()
                    retire(act_b + fstage_b)
                with ExitStack() as pfin:
                    FG = 4
                    NF = 3
                    finf = [sb(P_ + f"finf{i}", [128, FG, 512], F32, pfin) for i in range(NF)]
                    finf_b = fresh([Buf(f"finf{i}") for i in range(NF)])
                    finh = [sb(P_ + f"finh{i}", [128, FG, 512], F32, pfin) for i in range(NF)]
                    finh_b = fresh([Buf(f"finh{i}") for i in range(NF)])
                    for (sem, val) in DEAD.values():
                        k.wait(sp, (sem, val))
                    ngf = DC // FG

                    def load_fin(gi):
                        j = gi % NF
                        d0 = gi * FG
                        k.dma(sp, finf[j][:], f_scr[d0:d0 + FG, :, tok0:tok0 + T].rearrange("d p t -> p d t"),
                              finf_b[j], writes=[finf_b[j]])
                        k.dma(sp, finh[j][:], h_scr[d0:d0 + FG, :, tok0:tok0 + T].rearrange("d p t -> p d t"),
                              finh_b[j], writes=[finh_b[j]])
                    for gi in range(min(2, ngf)):
                        load_fin(gi)
                    for gi in range(ngf):
                        j = gi % NF
                        if gi + 2 < ngf:
                            load_fin(gi + 2)
                        for c in range(FG):
                            d = gi * FG + c
                            k.op(dve, lambda: nc.vector.scalar_tensor_tensor(
                                out=finf[j][:, c, :], in0=finf[j][:, c, :], scalar=gcol("po", d), in1=rf[:],
                                op0=ALU.mult, op1=ALU.mult), reads=[finf_b[j], rfb, gc_b], writes=[finf_b[j]])
                            k.op(dve, lambda: nc.vector.tensor_tensor(out=finf[j][:, c, :], in0=finf[j][:, c, :],
                                                                      in1=finh[j][:, c, :], op=ALU.add),
                                 reads=[finf_b[j], finh_b[j]], writes=[finf_b[j]])
                        d0 = gi * FG
                        k.dma(sp, outT[d0 * 128:(d0 + FG) * 128, tok0:tok0 + T].rearrange("(d p) t -> p d t", p=128),
                              finf[j][:], finf_b[j], reads=[finf_b[j]])
                    final_bufs = list(finf_b)
                    retire(finf_b + finh_b + [rfb])
                retire(gn_b + an_b)
        for b_ in final_bufs:
            k.wait(sp, b_.dlast)
        print(f"[kernel] arena peak={arena.peak/1024:.1f} KiB")
        print(f"[kernel] semaphores={k.nsem} waits={k.nwait} pe_groups={k.pe.cnt} act={k.act.cnt} dve={k.dve.cnt}")
    return nc


def host_inputs(C, x_b, pos_b, p, quarter):
    rot = quarter * C.OWN
    xr = np.roll(x_b, -rot, axis=0)
    pr = np.roll(pos_b, -rot, axis=0)
    xT = np.ascontiguousarray(xr.T)
    gc = np.zeros((128, 512), np.float32)
    o = 0
    for nm, n in (("pre_mix_norm", C.DC), ("q_norm", C.QC), ("kv_norm", C.KC), ("v_ln_gain", C.GH),
                  ("v_ln_bias", C.GH), ("attn_out_norm", C.AH), ("gmlp_out_norm", C.GH), ("post_mix_norm", C.DC),
                  ("pre_ffn_norm", C.DC), ("post_ffn_norm", C.DC)):
        gc[:, o:o + n] = p[nm].reshape(n, 128).T
        o += n
    return {"xT": xT, "pos": pr.reshape(1, -1).astype(np.int32), "gcols": gc}


def shared_inputs(C, p):
    wsT = np.ascontiguousarray(p["w_spatial"].transpose(2, 0, 1)).reshape(128, C.GH * 128)
    bsp = p["b_spatial"].reshape(C.GH, 128).astype(np.float32)
    inv = (1.0 / (10000.0 ** (np.arange(0, 64, 2, dtype=np.float32) / np.float32(64)))).astype(np.float32)
    rc = np.zeros((64, 4), np.float32)
    rc[:, 0] = np.concatenate([inv, inv])
    rc[:, 1] = np.concatenate([-np.ones(32, np.float32), np.ones(32, np.float32)])
    ident = np.eye(128, dtype=np.float32)
    return {"wsT": wsT.astype(np.float32), "bsp": bsp, "ropec": rc, "ident": ident}


_CACHE = {}


def run(C, x, positions, p, n_cores, debug=False):
    wstream, windex = build_wstream(C, p["w_in"], p["w_uq"], p["w_ukv"], p["w_out"], p["w_gate"], p["w_up"],
                                    p["w_down"])
    key = (C.D, C.SEQ, C.DFF, debug)
    if key not in _CACHE:
        _CACHE[key] = build_program(C, windex, debug=debug)
    nc = _CACHE[key]
    sh = shared_inputs(C, p)
    in_maps = []
    qpb = C.SEQ // C.OWN
    for c in range(n_cores):
        b, q = c // qpb, c % qpb
        m = host_inputs(C, x[b], positions[b], p, q)
        m.update(sh)
        m["wstream"] = wstream
        in_maps.append(m)
    res = run_bass_kernel_spmd(nc, in_maps, core_ids=list(range(n_cores)))
    out = np.empty(x.shape, np.float32)
    for c in range(n_cores):
        b, q = c // qpb, c % qpb
        out[b, q * C.OWN:(q + 1) * C.OWN, :] = res.results[c]["outT"].T
    return out, res


def kernel(x, positions, pre_mix_norm, w_in, q_norm, kv_norm, w_uq, w_ukv, v_ln_gain, v_ln_bias, w_spatial,
           b_spatial, attn_out_norm, gmlp_out_norm, w_out, post_mix_norm, pre_ffn_norm, w_gate, w_up, w_down,
           post_ffn_norm):
    C = Cfg()
    loc = dict(locals())
    p = {}
    for nm in ("pre_mix_norm", "w_in", "q_norm", "kv_norm", "w_uq", "w_ukv", "v_ln_gain", "v_ln_bias", "w_spatial",
               "b_spatial", "attn_out_norm", "gmlp_out_norm", "w_out", "post_mix_norm", "pre_ffn_norm", "w_gate",
               "w_up", "w_down", "post_ffn_norm"):
        p[nm] = np.asarray(loc[nm], dtype=np.float32)[0]
    x = np.asarray(x, dtype=np.float32)
    positions = np.asarray(positions)
    out, _ = run(C, x, positions, p, 8)
    return out
```

```python
import math
from contextlib import ExitStack

import numpy as np

import concourse.bass as bass
import concourse.mybir as mybir
from concourse.bass_utils import run_bass_kernel_spmd

F32 = mybir.dt.float32
BF16 = mybir.dt.bfloat16
I32 = mybir.dt.int32
AF = mybir.ActivationFunctionType
ALU = mybir.AluOpType
EPS = 1e-6
PI = math.pi


class Cfg:
    def __init__(self, D=4096, SEQ=4096, DFF=11008, QL=1024, KVL=512, AH=16, GH=16,
                 TILE_BLK=32, NRING=4):
        self.D, self.SEQ, self.DFF, self.QL, self.KVL, self.AH, self.GH = D, SEQ, DFF, QL, KVL, AH, GH
        self.T = 512
        self.NH = 2
        self.OWN = self.T * self.NH
        self.DC = D // 128
        self.QC = QL // 128
        self.KC = KVL // 128
        self.FC = DFF // 128
        self.GW = GH * 128
        self.AW = AH * 128
        assert self.AW + self.GW == D
        self.NBLK = SEQ // 512
        self.NKT = SEQ // 128
        self.TB = TILE_BLK
        self.TCOLS = TILE_BLK * 128
        self.NRING = NRING
        self.IN_W = QL + KVL + 64 + 2 * self.GW
        self.ARENA_BYTES = 206 * 1024
        self.FAST_RECIP = False
        self.POOL_Z = False
        self.ACT_RECIP = True
        self.DEFER_ATT = False
        self.DEFER_A = True


def ksplit(kc_total, tb):
    out, s = [], 0
    while s < kc_total:
        n = min(tb, kc_total - s)
        out.append((s, n))
        s += n
    return out


def build_wstream(C, w_in, w_uq, w_ukv, w_out, w_gate, w_up, w_down):
    tiles = []
    index = {}

    def add(name, arr2d):
        t = np.zeros((128, C.TCOLS), np.float32)
        t[:, :arr2d.shape[1]] = arr2d
        index.setdefault(name, []).append(len(tiles))
        tiles.append(t)

    def oc_major(W):
        K, N = W.shape
        return W.reshape(K // 128, 128, N // 128, 128).transpose(2, 1, 0, 3)

    def add_linear(name, W):
        A = oc_major(W)
        OC, _, KCn, _ = A.shape
        if KCn >= C.TB:
            for oc in range(OC):
                for (s, n) in ksplit(KCn, C.TB):
                    add(name, A[oc, :, s:s + n, :].reshape(128, n * 128))
        else:
            per = C.TB // KCn
            for o0 in range(0, OC, per):
                blk = A[o0:o0 + per]
                add(name, blk.transpose(1, 0, 2, 3).reshape(128, -1))

    QL, KVL, GW = C.QL, C.KVL, C.GW
    c_q = w_in[:, :QL]
    c_kv = w_in[:, QL:QL + KVL]
    c_r = w_in[:, QL + KVL:QL + KVL + 64]
    c_u = w_in[:, QL + KVL + 64:QL + KVL + 64 + GW]
    c_v = w_in[:, QL + KVL + 64 + GW:]
    c_rsw = np.concatenate([c_r[:, 32:], c_r[:, :32]], axis=1)
    add_linear("kv", np.concatenate([c_kv, c_r, c_rsw], axis=1))
    ukv = w_ukv.reshape(KVL, C.AH, 256)
    add_linear("ukvk", np.ascontiguousarray(ukv[:, :, :128]).reshape(KVL, C.AH * 128))
    Vp = np.ascontiguousarray(ukv[:, :, 128:]).reshape(KVL // 128, 128, C.AH * 128)
    Vp = Vp.transpose(1, 0, 2).reshape(128, -1)
    for s in range(0, Vp.shape[1], C.TCOLS):
        add("ukvv", Vp[:, s:s + C.TCOLS])
    add_linear("q", c_q)
    add_linear("v", c_v)
    add_linear("u", c_u)
    uq = w_uq.reshape(QL, C.AH, 192)
    uqp = np.concatenate([uq[:, :, :128], uq[:, :, 128:192], uq[:, :, 160:192], uq[:, :, 128:160]], axis=2)
    add_linear("uq", uqp.reshape(QL, C.AH * 256))
    add_linear("out", w_out)
    Ag, Au = oc_major(w_gate), oc_major(w_up)
    for fc in range(C.FC):
        for (s, n) in ksplit(C.DC, C.TB):
            add("gu", Ag[fc, :, s:s + n, :].reshape(128, n * 128))
        for (s, n) in ksplit(C.DC, C.TB):
            add("gu", Au[fc, :, s:s + n, :].reshape(128, n * 128))
    add_linear("down", w_down)
    return np.stack(tiles, 0), index


class Buf:
    __slots__ = ("name", "w", "r", "dsem", "dcnt", "dlast")

    def __init__(self, name):
        self.name = name
        self.w = None
        self.r = []
        self.dsem = None
        self.dcnt = 0
        self.dlast = None


class Eng:
    def __init__(self, raw, sem, name):
        self.raw, self.sem, self.name = raw, sem, name
        self.cnt = 0
        self.seen = {}


class K:
    def __init__(self, nc, es):
        self.nc, self.es = nc, es
        self.nsem = 0

        def mk(raw, name):
            return Eng(raw, self.newsem("e_" + name), name)
        self.pe = mk(nc.tensor, "pe")
        self.act = mk(nc.scalar, "act")
        self.dve = mk(nc.vector, "dve")
        self.pool = mk(nc.gpsimd, "pool")
        self.sp = mk(nc.sync, "sp")
        self.nwait = 0
        self.dsems = {}

    def newsem(self, name):
        self.nsem += 1
        return self.es.enter_context(self.nc.semaphore(name))

    def wait(self, eng, ev):
        if ev is None:
            return
        sem, val = ev
        key = id(sem)
        if eng is self.pe and sem is self.pe.sem:
            return
        if eng.seen.get(key, 0) >= val:
            return
        eng.raw.wait_ge(sem, val)
        eng.seen[key] = val
        self.nwait += 1

    def deps(self, eng, reads, writes):
        for b in reads:
            self.wait(eng, b.w)
        for b in writes:
            self.wait(eng, b.w)
            for r in b.r:
                self.wait(eng, r)

    def commit(self, ev, reads, writes):
        for b in reads:
            b.r.append(ev)
        for b in writes:
            b.w = ev
            b.r = []

    def op(self, eng, fn, reads=(), writes=()):
        self.deps(eng, reads, writes)
        ins = fn()
        eng.cnt += 1
        ins.then_inc(eng.sem, 1)
        ev = (eng.sem, eng.cnt)
        self.commit(ev, reads, writes)
        return ins

    def group(self, eng, fns, reads=(), writes=()):
        self.deps(eng, reads, writes)
        ins = None
        for fn in fns:
            ins = fn()
        eng.cnt += 1
        ins.then_inc(eng.sem, 1)
        ev = (eng.sem, eng.cnt)
        self.commit(ev, reads, writes)
        return ins

    def dma(self, q, out, in_, sb, reads=(), writes=()):
        if sb.dsem is None:
            if sb.name not in self.dsems:
                self.dsems[sb.name] = [self.newsem("d_" + sb.name), 0, None]
            sb.dsem = self.dsems[sb.name]
        rec = sb.dsem
        self.deps(q, reads, writes)
        self.wait(q, rec[2])
        ins = q.raw.dma_start(out=out, in_=in_)
        rec[1] += 16
        ins.then_inc(rec[0], 16)
        ev = (rec[0], rec[1])
        rec[2] = ev
        sb.dlast = ev
        self.commit(ev, reads, writes)
        return ins


DEAD = {}


def retire(bufs):
    for b in bufs:
        evs = list(b.r)
        if b.w is not None:
            evs.append(b.w)
        if b.dlast is not None:
            evs.append(b.dlast)
        for (sem, val) in evs:
            key = id(sem)
            if key not in DEAD or DEAD[key][1] < val:
                DEAD[key] = (sem, val)


def fresh(bufs):
    evs = list(DEAD.values())
    for b in bufs:
        b.r = list(evs) + b.r
    return bufs


def alias_after(new_bufs, old_bufs):
    evs = []
    for b in old_bufs:
        if b.w is not None:
            evs.append(b.w)
        evs.extend(b.r)
    for b in new_bufs:
        b.r = list(evs) + b.r


class Arena:
    def __init__(self, nc, es, nbytes):
        self.t = es.enter_context(nc.sbuf_tensor("arena", [128, nbytes // 2], BF16))
        self.free_list = [(0, nbytes)]
        self.peak = 0
        self.used = 0

    def alloc(self, n):
        n = (n + 63) // 64 * 64
        for i, (o, sz) in enumerate(self.free_list):
            if sz >= n:
                if sz == n:
                    self.free_list.pop(i)
                else:
                    self.free_list[i] = (o + n, sz - n)
                self.used += n
                self.peak = max(self.peak, self.used)
                return o, n
        raise MemoryError(f"SBUF arena exhausted: need {n}, free {self.free_list}")

    def free(self, o, n):
        self.used -= n
        fl = sorted(self.free_list + [(o, n)])
        out = []
        for (a, b) in fl:
            if out and out[-1][0] + out[-1][1] == a:
                out[-1] = (out[-1][0], out[-1][1] + b)
            else:
                out.append((a, b))
        self.free_list = out

    def view(self, name, shape, dt, stack):
        P = shape[0]
        cols = 1
        for d in shape[1:]:
            cols *= d
        esz = 2 if dt == BF16 else 4
        o, n = self.alloc(cols * esz)
        stack.callback(self.free, o, n)
        v = self.t[0:P, o // 2:(o + cols * esz) // 2]
        if dt != BF16:
            v = v.bitcast(dt)
        if len(shape) == 3:
            v = v.rearrange("p (a b) -> p a b", a=shape[1])
        return v


def build_program(C, windex, debug=False):
    DEAD.clear()
    nc = bass.Bass("TRN2", target_bir_lowering=False)
    T, DC, QC, KC, FC, AH, GH = C.T, C.DC, C.QC, C.KC, C.FC, C.AH, C.GH
    SEQ, OWN = C.SEQ, C.OWN
    NT = sum(len(v) for v in windex.values())
    scr_kind = "ExternalOutput"
    assert DC == C.TB

    xT = nc.dram_tensor("xT", [C.D, SEQ], F32, kind="ExternalInput").ap()
    pos = nc.dram_tensor("pos", [1, SEQ], I32, kind="ExternalInput").ap()
    wst = nc.dram_tensor("wstream", [NT, 128, C.TCOLS], F32, kind="ExternalInput").ap()
    gcols = nc.dram_tensor("gcols", [128, 512], F32, kind="ExternalInput").ap()
    wsT_d = nc.dram_tensor("wsT", [128, GH * 128], F32, kind="ExternalInput").ap()
    bsp_d = nc.dram_tensor("bsp", [GH, 128], F32, kind="ExternalInput").ap()
    ropec = nc.dram_tensor("ropec", [64, 4], F32, kind="ExternalInput").ap()
    ident_d = nc.dram_tensor("ident", [128, 128], F32, kind="ExternalInput").ap()
    outT = nc.dram_tensor("outT", [C.D, OWN], F32, kind="ExternalOutput").ap()
    kT_scr = nc.dram_tensor("kT_scr", [AH, 128, SEQ], BF16, kind=scr_kind).ap()
    v_scr = nc.dram_tensor("v_scr", [AH, 128, SEQ], BF16, kind=scr_kind).ap()
    kr_scr = nc.dram_tensor("kr_scr", [64, SEQ], BF16, kind=scr_kind).ap()
    h_scr = nc.dram_tensor("h_scr", [DC, 128, OWN], F32, kind=scr_kind).ap()
    f_scr = nc.dram_tensor("f_scr", [DC, 128, OWN], F32, kind=scr_kind).ap()
    b_scr = nc.dram_tensor("b_scr", [2, GH * 128], BF16, kind=scr_kind).ap()

    off = {}
    o = 0
    for nm, n in (("pre", DC), ("q", QC), ("kv", KC), ("vg", GH), ("vb", GH), ("an", AH), ("gn", GH),
                  ("pm", DC), ("pf", DC), ("po", DC)):
        off[nm] = o
        o += n
    assert o <= 512

    with ExitStack() as es:
        k = K(nc, es)
        pe, act, dve, pool, sp = k.pe, k.act, k.dve, k.pool, k.sp

        arena = Arena(nc, es, C.ARENA_BYTES)

        def sb(name, shape, dt, stack=None):
            return arena.view(name, shape, dt, stack if stack is not None else es)

        ring = [sb(f"ring{i}", [128, C.TCOLS], BF16) for i in range(C.NRING)]
        ring_b = [Buf(f"ring{i}") for i in range(C.NRING)]
        banks = [es.enter_context(nc.psum_tensor(f"bank{i}", [128, 512], F32)) for i in range(8)]
        bank_b = [Buf(f"bank{i}") for i in range(8)]
        gc = sb("gc", [128, 512], F32)
        gc_b = Buf("gc")
        ident = sb("ident", [128, 128], BF16)
        ones = sb("ones", [128, 128], BF16)
        ones32 = sb("ones32", [128, 128], F32)
        wsT = sb("wsT", [128, GH * 128], BF16)
        bhl = sb("bhl", [2, GH * 128], BF16)
        b16 = sb("b16", [GH, 128], F32)
        bhi16 = sb("bhi16", [GH, 128], BF16)
        blo16 = sb("blo16", [GH, 128], BF16)
        bt16 = sb("bt16", [GH, 128], F32)
        rc = sb("rc", [64, 4], F32)
        const_b = Buf("const")

        k.dma(sp, gc[:], gcols, gc_b, writes=[gc_b])
        k.dma(sp, rc[:], ropec, const_b, writes=[const_b])
        b16b = Buf("b16")
        k.dma(sp, b16[:], bsp_d, b16b, writes=[b16b])
        identb = Buf("ident")
        k.dma(pool, ident[:], ident_d, identb, writes=[identb])
        wsTb = Buf("wsT")
        k.dma(pool, wsT[:], wsT_d, wsTb, writes=[wsTb])
        onesb = Buf("ones")
        k.op(dve, lambda: nc.vector.memset(ones[:], 1.0), writes=[onesb])
        k.op(dve, lambda: nc.vector.memset(ones32[:], 1.0), writes=[onesb])
        bhib, blob, btb, bhlb = Buf("bhi"), Buf("blo"), Buf("bt"), Buf("bhl")
        k.op(dve, lambda: nc.vector.tensor_copy(out=bhi16[:], in_=b16[:]), reads=[b16b], writes=[bhib])
        k.op(dve, lambda: nc.vector.tensor_tensor(out=bt16[:], in0=b16[:], in1=bhi16[:], op=ALU.subtract),
             reads=[b16b, bhib], writes=[btb])
        k.op(dve, lambda: nc.vector.tensor_copy(out=blo16[:], in_=bt16[:]), reads=[btb], writes=[blob])
        k.dma(sp, b_scr[0:1, :].rearrange("o (g p) -> (o g) p", g=GH), bhi16[:], bhib, reads=[bhib])
        k.dma(sp, b_scr[1:2, :].rearrange("o (g p) -> (o g) p", g=GH), blo16[:], blob, reads=[blob])
        k.wait(sp, bhib.dlast)
        k.wait(sp, blob.dlast)
        k.dma(sp, bhl[:], b_scr, bhlb, writes=[bhlb])

        def gcol(nm, i, p=128):
            return gc[0:p, off[nm] + i: off[nm] + i + 1]

        wpos = {"i": 0}
        wcursor = {n: 0 for n in windex}

        def wtile(name):
            lst = windex[name]
            idx = lst[wcursor[name] % len(lst)]
            wcursor[name] += 1
            s = wpos["i"] % C.NRING
            wpos["i"] += 1
            k.dma(pool, ring[s][:], wst[idx], ring_b[s], writes=[ring_b[s]])
            return ring[s], ring_b[s]

        bstate = {"i": 0, "reserved": set()}

        def next_bank():
            while True:
                b = bstate["i"] % 8
                bstate["i"] += 1
                if b not in bstate["reserved"]:
                    return b

        def reserve(n):
            got = []
            while len(got) < n:
                b = next_bank()
                bstate["reserved"].add(b)
                got.append(b)
            return got

        def release(bs):
            for b in bs:
                bstate["reserved"].discard(b)

        def mm(out, lhsT, rhs, start, stop):
            return lambda: nc.tensor.matmul(out, lhsT=lhsT, rhs=rhs, start=start, stop=stop)

        NW = 4
        wk = [sb(f"wk{i}", [128, 512], F32) for i in range(NW)]
        wk_b = [Buf(f"wk{i}") for i in range(NW)]
        wkc = {"i": 0}

        def work():
            i = wkc["i"] % NW
            wkc["i"] += 1
            return wk[i], wk_b[i]
        NS = 4
        sqt = [sb(f"sq{i}", [128, 512], BF16) for i in range(NS)]
        sq_b = [Buf(f"sq{i}") for i in range(NS)]
        sqc = {"i": 0}

        def rep_tile(stack, name):
            t = sb("rep_" + name + f"_{wpos['i']}_{k.pe.cnt}", [128, 512], F32, stack)
            b = fresh([Buf("rep_" + name)])[0]
            return t, b

        def recip(out_ap, in_ap, rbufs, wbufs):
            if C.FAST_RECIP:
                sc, scb = work()
                k.op(dve, lambda: nc.vector.reciprocal_approx_accurate(out=out_ap, in_=in_ap, scratch=sc[:]),
                     reads=list(rbufs), writes=list(wbufs) + [scb])
            else:
                k.op(dve, lambda: nc.vector.reciprocal(out=out_ap, in_=in_ap), reads=list(rbufs), writes=list(wbufs))

        def rstd_from(bank_i, n, out_ap, out_b):
            run_deferred()
            run_deferred()
            tmp, tmp_b = work()
            k.op(act, lambda: nc.scalar.activation(out=tmp[:], in_=banks[bank_i][:], func=AF.Sqrt,
                                                   bias=EPS, scale=1.0 / n),
                 reads=[bank_b[bank_i]], writes=[tmp_b])
            recip(out_ap, tmp[:], [tmp_b], [out_b])

        deferred = []

        def defer_pe(fn):
            deferred.append(fn)

        def run_deferred():
            for _ in range(len(deferred)):
                deferred.pop(0)()

        class Stat:
            def __init__(self, n_groups):
                run_deferred()
                self.bank = reserve(1)[0]
                self.n = n_groups
                self.i = 0

            def add_sq_of(self, src_ap, src_bufs, now=False):
                if len(deferred) >= 2:
                    run_deferred()
                j = sqc["i"] % NS
                sqc["i"] += 1
                k.op(act, lambda: nc.scalar.activation(out=sqt[j][:], in_=src_ap, func=AF.Square),
                     reads=src_bufs, writes=[sq_b[j]])
                self.add(sqt[j][:], [sq_b[j]], now)

            def add(self, ap, bufs, now=False):
                b = self.bank
                st_, sp_ = self.i == 0, self.i == self.n - 1
                self.i += 1
                bl = list(bufs)

                def go():
                    k.group(pe, [mm(banks[b][:], ones[:], ap, st_, sp_)], reads=bl + [onesb], writes=[bank_b[b]])
                if now:
                    go()
                else:
                    defer_pe(go)

            def done(self):
                run_deferred()
                run_deferred()
                assert self.i == self.n
                release([self.bank])

        def linear(wname, n_oc, kc_total, rhs_fn, rhs_bufs, epilogue):
            if kc_total >= C.TB:
                pieces = ksplit(kc_total, C.TB)
                for oc in range(n_oc):
                    b = next_bank()
                    first = True
                    for (s, n) in pieces:
                        wt, wb = wtile(wname)
                        fns = []
                        for j in range(n):
                            kc = s + j
                            fns.append(mm(banks[b][:], wt[:, j * 128:(j + 1) * 128], rhs_fn(kc),
                                          first and j == 0, kc == kc_total - 1))
                        k.group(pe, fns, reads=[wb] + rhs_bufs(s, n), writes=[bank_b[b]])
                        first = False
                    run_deferred()
                    epilogue(oc, b)
            else:
                per = C.TB // kc_total
                for o0 in range(0, n_oc, per):
                    wt, wb = wtile(wname)
                    for ol in range(min(per, n_oc - o0)):
                        b = next_bank()
                        fns = []
                        for kc in range(kc_total):
                            c0 = (ol * kc_total + kc) * 128
                            fns.append(mm(banks[b][:], wt[:, c0:c0 + 128], rhs_fn(kc), kc == 0,
                                          kc == kc_total - 1))
                        k.group(pe, fns, reads=[wb] + rhs_bufs(0, kc_total), writes=[bank_b[b]])
                        run_deferred()
                        epilogue(o0 + ol, b)

        def rope_tables(tok0, cos2, sin2, cs_b, stack):
            names = (("pi", I32), ("a", F32), ("t", F32), ("i", I32), ("r", F32), ("m", F32))
            pi_t, a_t, t_t, i_t, r_t, m_t = [sb(f"rs_{n}_{tok0}_{k.dve.cnt}", [64, 512], dt, stack)
                                             for n, dt in names]
            sb_ = fresh([Buf("ropescr")])[0]
            k.dma(sp, pi_t[:], pos[0:1, tok0:tok0 + 512].partition_broadcast(64), sb_, writes=[sb_])
            V = nc.vector

            def d(fn):
                k.op(dve, fn, reads=[sb_, const_b], writes=[sb_])
            d(lambda: V.tensor_copy(out=a_t[:], in_=pi_t[:]))
            d(lambda: V.tensor_scalar(out=a_t[:], in0=a_t[:], scalar1=rc[:, 0:1], scalar2=None, op0=ALU.mult))
            for which, dst in ((0, sin2), (1, cos2)):
                shift = 0.0 if which == 0 else PI / 2
                d(lambda: V.tensor_scalar(out=t_t[:], in0=a_t[:], scalar1=shift, scalar2=1.0 / (2 * PI),
                                          op0=ALU.add, op1=ALU.mult))
                d(lambda: V.tensor_scalar(out=t_t[:], in0=t_t[:], scalar1=0.5, scalar2=None, op0=ALU.add))
                d(lambda: V.tensor_copy(out=i_t[:], in_=t_t[:]))
                d(lambda: V.tensor_copy(out=t_t[:], in_=i_t[:]))
                C1 = 6.28125
                C2 = 2 * PI - C1
                d(lambda: V.scalar_tensor_tensor(out=r_t[:], in0=t_t[:], scalar=-C1, in1=a_t[:],
                                                 op0=ALU.mult, op1=ALU.add))
                d(lambda: V.scalar_tensor_tensor(out=r_t[:], in0=t_t[:], scalar=-C2, in1=r_t[:],
                                                 op0=ALU.mult, op1=ALU.add))
                if shift:
                    d(lambda: V.tensor_scalar(out=r_t[:], in0=r_t[:], scalar1=shift, scalar2=None, op0=ALU.add))
                d(lambda: V.tensor_scalar(out=m_t[:], in0=r_t[:], scalar1=-PI, scalar2=2 * PI,
                                          op0=ALU.is_lt, op1=ALU.mult))
                d(lambda: V.tensor_tensor(out=r_t[:], in0=r_t[:], in1=m_t[:], op=ALU.add))
                d(lambda: V.tensor_scalar(out=m_t[:], in0=r_t[:], scalar1=PI, scalar2=-2 * PI,
                                          op0=ALU.is_gt, op1=ALU.mult))
                d(lambda: V.tensor_tensor(out=r_t[:], in0=r_t[:], in1=m_t[:], op=ALU.add))
                d(lambda: V.tensor_scalar(out=r_t[:], in0=r_t[:], scalar1=3.1415925, scalar2=-3.1415925,
                                          op0=ALU.min, op1=ALU.max))
                k.op(act, lambda: nc.scalar.activation(out=dst, in_=r_t[:], func=AF.Sin), reads=[sb_],
                     writes=[cs_b])
            k.op(dve, lambda: V.tensor_scalar(out=sin2, in0=sin2, scalar1=rc[:, 1:2], scalar2=None, op0=ALU.mult),
                 reads=[cs_b, const_b], writes=[cs_b])
            return [sb_]

        def apply_rope(bankA, bankB, cos2, sin2, cs_b, scale_rep, out_ap, out_bufs):
            t1, t1b = work()
            t2, t2b = work()
            V = nc.vector
            k.op(dve, lambda: V.tensor_tensor(out=t1[0:64, :], in0=banks[bankA][0:64, :], in1=cos2[:], op=ALU.mult),
                 reads=[bank_b[bankA], cs_b], writes=[t1b])
            k.op(dve, lambda: V.tensor_tensor(out=t2[0:64, :], in0=banks[bankB][0:64, :], in1=sin2[:], op=ALU.mult),
                 reads=[bank_b[bankB], cs_b], writes=[t2b])
            if scale_rep is None:
                k.op(dve, lambda: V.tensor_tensor(out=out_ap, in0=t1[0:64, :], in1=t2[0:64, :], op=ALU.add),
                     reads=[t1b, t2b], writes=out_bufs)
            else:
                sr, srb = scale_rep
                k.op(dve, lambda: V.tensor_tensor(out=t1[0:64, :], in0=t1[0:64, :], in1=t2[0:64, :], op=ALU.add),
                     reads=[t1b, t2b], writes=[t1b])
                k.op(dve, lambda: V.tensor_tensor(out=out_ap, in0=t1[0:64, :], in1=sr[0:64, :], op=ALU.mult),
                     reads=[t1b, srb], writes=out_bufs)

        XG = 4

        def make_xg_gen(tok0, xg, xg_b, xbuf, xbuf_b, rx, rxb, now=True):
            XGl = xbuf[0].shape[1]

            def load_x(gi):
                j = gi % len(xbuf)
                src = xT[gi * XGl * 128:(gi + 1) * XGl * 128, tok0:tok0 + 512].rearrange("(g p) t -> p g t", p=128)
                k.dma(sp, xbuf[j][:], src, xbuf_b[j], writes=[xbuf_b[j]])
                return xbuf[j], xbuf_b[j]
            st = Stat(DC)
            ng = DC // XGl
            nb_ = len(xbuf)
            dist = nb_ - 1
            loaded = {}
            for gi in range(min(dist, ng)):
                loaded[gi] = load_x(gi)
            for gi in range(ng):
                if gi + dist < ng:
                    loaded[gi + dist] = load_x(gi + dist)
                xb_, xbb = loaded.pop(gi)
                for g in range(XGl):
                    fc = gi * XGl + g
                    st.add_sq_of(xb_[:, g, :], [xbb], now=now)
                    k.op(dve, lambda: nc.vector.tensor_scalar(out=xg[:, fc * 512:(fc + 1) * 512], in0=xb_[:, g, :],
                                                              scalar1=gcol("pre", fc), scalar2=None, op0=ALU.mult),
                         reads=[xbb, gc_b], writes=[xg_b[fc]])
                    yield
            rstd_from(st.bank, C.D, rx[:], rxb)
            st.done()

        def make_xg(*a):
            for _ in make_xg_gen(*a):
                pass

        with ExitStack() as pa:
            xg = sb("A_xg", [128, DC * 512], BF16, pa)
            xg_b = [Buf(f"A_xg{i}") for i in range(DC)]
            xbuf = [sb(f"A_xbuf{i}", [128, XG, 512], F32, pa) for i in range(3)]
            xbuf_b = [Buf(f"A_xbuf{i}") for i in range(3)]
            cosA = [sb(f"A_cos2_{i}", [64, 512], F32, pa) for i in range(2)]
            sinA = [sb(f"A_sin2_{i}", [64, 512], F32, pa) for i in range(2)]
            csA_b = [Buf(f"A_cossin{i}") for i in range(2)]
            nuk = len(windex["ukvk"])
            nuv = len(windex["ukvv"])
            ukw = [sb(f"A_ukw{i}", [128, C.TCOLS], BF16, pa) for i in range(nuk + nuv)]
            ukw_b = [Buf(f"A_ukw{i}") for i in range(nuk + nuv)]
            for i in range(nuk):
                k.dma(pool, ukw[i][:], wst[windex["ukvk"][i]], ukw_b[i], writes=[ukw_b[i]])
            for i in range(nuv):
                k.dma(pool, ukw[nuk + i][:], wst[windex["ukvv"][i]], ukw_b[nuk + i], writes=[ukw_b[nuk + i]])
            kvc = sb("A_kvc", [128, KC * 512], F32, pa)
            kvc_b = [Buf(f"A_kvc{i}") for i in range(KC)]
            kvn = sb("A_kvn", [128, KC * 512], BF16, pa)
            kvn_b = [Buf(f"A_kvn{i}") for i in range(KC)]
            kst = [sb(f"A_kst{i}", [128, 4 * 512], BF16, pa) for i in range(2)]
            kst_b = [Buf(f"A_kst{i}") for i in range(2)]
            vst = [sb(f"A_vst{i}", [128, 4 * 512], BF16, pa) for i in range(2)]
            vst_b = [Buf(f"A_vst{i}") for i in range(2)]
            krs = sb("A_krs", [64, 512], BF16, pa)
            krs_b = Buf("A_krs")
            rx, rxb = rep_tile(pa, "rx")
            r2, r2b = rep_tile(pa, "r2")
            ropeb = []
            scr_ev = []

            def ropeprep(blk_):
                with ExitStack() as prs:
                    rb_ = rope_tables(blk_ * 512, cosA[blk_ % 2][:], sinA[blk_ % 2][:], csA_b[blk_ % 2], prs)
                retire(rb_)

            def prep(blk_):
                return make_xg_gen(blk_ * 512, xg, xg_b, xbuf, xbuf_b, rx, rxb, now=(blk_ == 0 or not C.DEFER_A))
            ropeprep(0)
            for _ in prep(0):
                pass
            for blk in range(C.NBLK):
                tok0 = blk * 512
                cos2, sin2, cs_b = cosA[blk % 2], sinA[blk % 2], csA_b[blk % 2]
                if blk + 1 < C.NBLK:
                    ropeprep(blk + 1)

                def ep_lat(oc, b):
                    k.op(dve, lambda: nc.vector.tensor_tensor(out=kvc[:, oc * 512:(oc + 1) * 512], in0=banks[b][:],
                                                              in1=rx[:], op=ALU.mult),
                         reads=[bank_b[b], rxb], writes=[kvc_b[oc]])
                linear("kv", KC, DC, lambda kc: xg[:, kc * 512:(kc + 1) * 512], lambda s, n: xg_b[s:s + n], ep_lat)
                wt, wb = wtile("kv")
                bA, bB = next_bank(), next_bank()
                for (bb, cofs) in ((bA, 0), (bB, 64)):
                    fns = []
                    for kc in range(DC):
                        fns.append(mm(banks[bb][0:64, :], wt[:, kc * 128 + cofs:kc * 128 + cofs + 64],
                                      xg[:, kc * 512:(kc + 1) * 512], kc == 0, kc == DC - 1))
                    k.group(pe, fns, reads=[wb] + xg_b, writes=[bank_b[bb]])
                apply_rope(bA, bB, cos2, sin2, cs_b, (rx, rxb), krs[:], [krs_b])
                k.dma(sp, kr_scr[:, tok0:tok0 + 512], krs[:], krs_b, reads=[krs_b])
                st = Stat(KC)
                for oc in range(KC):
                    st.add_sq_of(kvc[:, oc * 512:(oc + 1) * 512], [kvc_b[oc]])
                rstd_from(st.bank, C.KVL, r2[:], r2b)
                st.done()
                for oc in range(KC):
                    k.op(dve, lambda: nc.vector.scalar_tensor_tensor(
                        out=kvn[:, oc * 512:(oc + 1) * 512], in0=kvc[:, oc * 512:(oc + 1) * 512],
                        scalar=gcol("kv", oc), in1=r2[:], op0=ALU.mult, op1=ALU.mult),
                        reads=[kvc_b[oc], r2b, gc_b], writes=[kvn_b[oc]])
                gen = prep(blk + 1) if blk + 1 < C.NBLK else iter(())

                def step():
                    next(gen, None)
                per = max(1, C.TB // KC)
                for hg in range(AH // 4):
                    sj = hg % 2
                    for hl in range(4):
                        h = hg * 4 + hl
                        b = next_bank()
                        t_, b_ = ukw[h // per], ukw_b[h // per]
                        fns = []
                        for kc in range(KC):
                            c0 = ((h % per) * KC + kc) * 128
                            fns.append(mm(banks[b][:], t_[:, c0:c0 + 128], kvn[:, kc * 512:(kc + 1) * 512], kc == 0,
                                          kc == KC - 1))
                        k.group(pe, fns, reads=[b_] + kvn_b, writes=[bank_b[b]])
                        run_deferred()
                        k.op(act, lambda: nc.scalar.copy(out=kst[sj][:, hl * 512:(hl + 1) * 512], in_=banks[b][:]),
                             reads=[bank_b[b]], writes=[kst_b[sj]])
                        step()
                    kdst = kT_scr[hg * 4:(hg + 1) * 4, :, tok0:tok0 + 512].rearrange("h p t -> p h t")
                    k.dma(sp, kdst, kst[sj][:].rearrange("p (h t) -> p h t", h=4), kst_b[sj], reads=[kst_b[sj]])
                vcols = AH * 128
                for hg in range(AH // 4):
                    sj = hg % 2
                    vv = vst[sj][:].rearrange("p (h t d) -> p h t d", h=4, t=4)
                    for tt in range(4):
                        b = next_bank()
                        fns, rd = [], []
                        for kc in range(KC):
                            colg = kc * vcols + hg * 512
                            ti, c0 = colg // C.TCOLS, colg % C.TCOLS
                            if ukw_b[nuk + ti] not in rd:
                                rd.append(ukw_b[nuk + ti])
                            fns.append(mm(banks[b][:], kvn[:, kc * 512 + tt * 128: kc * 512 + (tt + 1) * 128],
                                          ukw[nuk + ti][:, c0:c0 + 512], kc == 0, kc == KC - 1))
                        k.group(pe, fns, reads=rd + kvn_b, writes=[bank_b[b]])
                        run_deferred()
                        src = banks[b][:].rearrange("p (h d) -> p h d", h=4)
                        dstv = vv[:, :, tt, :]
                        if tt % 2 == 0:
                            k.op(act, lambda: nc.scalar.copy(out=dstv, in_=src), reads=[bank_b[b]],
                                 writes=[vst_b[sj]])
                        else:
                            k.op(dve, lambda: nc.vector.tensor_copy(out=dstv, in_=src), reads=[bank_b[b]],
                                 writes=[vst_b[sj]])
                        step()
                    vdst = v_scr[hg * 4:(hg + 1) * 4, :, tok0:tok0 + 512].rearrange("h p t -> p h t")
                    k.dma(sp, vdst, vst[sj][:].rearrange("p (h t) -> p h t", h=4), vst_b[sj], reads=[vst_b[sj]])
                for _ in gen:
                    pass
            retire(xg_b + xbuf_b + csA_b + ukw_b + kvc_b + kvn_b + kst_b + vst_b + [krs_b, rxb, r2b])
        scr_guard = fresh([Buf("scr_guard")])[0]

        scale = 1.0 / math.sqrt(192.0)
        for half in range(C.NH):
            tok0 = half * T
            with ExitStack() as ph:
                P_ = f"h{half}_"
                gnT = sb(P_ + "gnT", [128, GH * 512], BF16, ph)
                gn_b = fresh([Buf(f"gn{i}") for i in range(GH)])
                anT = sb(P_ + "anT", [128, AH * 512], BF16, ph)
                an_b = fresh([Buf(f"an{i}") for i in range(AH)])
                ra, rab = rep_tile(ph, "ra")
                pqn = ExitStack()
                qn = sb(P_ + "qn", [128, QC * 512], BF16, pqn)
                qn_b = fresh([Buf(f"qn{i}") for i in range(QC)])
                pcs = ExitStack()
                cos2 = sb(P_ + "cos2", [64, 512], F32, pcs)
                sin2 = sb(P_ + "sin2", [64, 512], F32, pcs)
                cs_b = fresh([Buf("cossin")])[0]
                with ExitStack() as prs:
                    rb_ = rope_tables(tok0, cos2[:], sin2[:], cs_b, prs)
                retire(rb_)

                with ExitStack() as pb:
                    xg = sb(P_ + "xg", [128, DC * 512], BF16, pb)
                    xg_b = fresh([Buf(f"B_xg{i}") for i in range(DC)])
                    xbuf = [sb(P_ + f"xbuf{i}", [128, 2, 512], F32, pb) for i in range(3)]
                    xbuf_b = fresh([Buf(f"xbuf{i}") for i in range(3)])
                    rx, rxb = rep_tile(pb, "rx")
                    make_xg(tok0, xg, xg_b, xbuf, xbuf_b, rx, rxb)

                    def rhs_x(kc):
                        return xg[:, kc * 512:(kc + 1) * 512]

                    def rhsb_x(s, n):
                        return xg_b[s:s + n]

                    with ExitStack() as pq:
                        qc = sb(P_ + "qc", [128, QC * 512], F32, pq)
                        qc_b = fresh([Buf(f"qc{i}") for i in range(QC)])
                        rq, rqb = rep_tile(pq, "rq")
                        st = Stat(QC)

                        def ep_q(oc, b):
                            k.op(dve, lambda: nc.vector.tensor_tensor(out=qc[:, oc * 512:(oc + 1) * 512],
                                                                      in0=banks[b][:], in1=rx[:], op=ALU.mult),
                                 reads=[bank_b[b], rxb], writes=[qc_b[oc]])
                            st.add_sq_of(qc[:, oc * 512:(oc + 1) * 512], [qc_b[oc]])
                        linear("q", QC, DC, rhs_x, rhsb_x, ep_q)
                        rstd_from(st.bank, C.QL, rq[:], rqb)
                        st.done()
                        for oc in range(QC):
                            k.op(dve, lambda: nc.vector.scalar_tensor_tensor(
                                out=qn[:, oc * 512:(oc + 1) * 512], in0=qc[:, oc * 512:(oc + 1) * 512],
                                scalar=gcol("q", oc), in1=rq[:], op0=ALU.mult, op1=ALU.mult),
                                reads=[qc_b[oc], rqb, gc_b], writes=[qn_b[oc]])
                        retire(qc_b + [rqb])

                    vt = sb(P_ + "vt", [128, GH * 512], F32, pb)
                    vt_b = fresh([Buf(f"vt{i}") for i in range(GH)])
                    vb16 = [sb(P_ + f"vb16_{i}", [128, 512], BF16, pb) for i in range(2)]
                    vb16_b = fresh([Buf(f"vb16_{i}") for i in range(2)])
                    mean, meanb = rep_tile(pb, "mean")
                    rv, rvb = rep_tile(pb, "rv")
                    s1 = reserve(1)[0]
                    st2 = Stat(GH)
                    vcnt = {"i": 0}

                    def ep_v(oc, b):
                        t, tb = work()
                        k.op(dve, lambda: nc.vector.tensor_tensor(out=t[:], in0=banks[b][:], in1=rx[:], op=ALU.mult),
                             reads=[bank_b[b], rxb], writes=[tb])
                        k.op(act, lambda: nc.scalar.activation(out=vt[:, oc * 512:(oc + 1) * 512], in_=t[:],
                                                               func=AF.Gelu_apprx_tanh), reads=[tb], writes=[vt_b[oc]])
                        j = vcnt["i"] % 2
                        vcnt["i"] += 1
                        k.op(dve, lambda: nc.vector.tensor_copy(out=vb16[j][:], in_=vt[:, oc * 512:(oc + 1) * 512]),
                             reads=[vt_b[oc]], writes=[vb16_b[j]])
                        defer_pe(lambda j=j, oc=oc: k.group(
                            pe, [mm(banks[s1][:], ones[:], vb16[j][:], oc == 0, oc == GH - 1)],
                            reads=[vb16_b[j], onesb], writes=[bank_b[s1]]))
                        st2.add_sq_of(vt[:, oc * 512:(oc + 1) * 512], [vt_b[oc]])
                    linear("v", GH, DC, rhs_x, rhsb_x, ep_v)
                    run_deferred()
                    t, tb = work()
                    t2, t2b = work()
                    k.op(dve, lambda: nc.vector.tensor_scalar(out=mean[:], in0=banks[s1][:], scalar1=1.0 / C.GW,
                                                              scalar2=None, op0=ALU.mult),
                         reads=[bank_b[s1]], writes=[meanb])
                    k.op(dve, lambda: nc.vector.tensor_tensor(out=t[:], in0=mean[:], in1=mean[:], op=ALU.mult),
                         reads=[meanb], writes=[tb])
                    k.op(dve, lambda: nc.vector.scalar_tensor_tensor(out=t2[:], in0=banks[st2.bank][:],
                                                                     scalar=1.0 / C.GW, in1=t[:], op0=ALU.mult,
                                                                     op1=ALU.subtract),
                         reads=[bank_b[st2.bank], tb], writes=[t2b])
                    k.op(dve, lambda: nc.vector.tensor_scalar(out=t2[:], in0=t2[:], scalar1=0.0, scalar2=None,
                                                              op0=ALU.max), reads=[t2b], writes=[t2b])
                    k.op(act, lambda: nc.scalar.activation(out=t[:], in_=t2[:], func=AF.Sqrt, bias=EPS, scale=1.0),
                         reads=[t2b], writes=[tb])
                    recip(rv[:], t[:], [tb], [rvb])
                    release([s1])
                    st2.done()
                    vln = sb(P_ + "vln", [128, 4 * C.GW], BF16, pb)
                    vln_b = fresh([Buf(f"vln{i}") for i in range(GH)])
                    vln_v = vln[:].rearrange("p (t c) -> p t c", t=4)

                    def ln_apply(c):
                        t, tb = work()
                        k.op(dve, lambda: nc.vector.tensor_tensor(out=t[:], in0=vt[:, c * 512:(c + 1) * 512],
                                                                  in1=mean[:], op=ALU.subtract),
                             reads=[vt_b[c], meanb], writes=[tb])
                        k.op(dve, lambda: nc.vector.tensor_tensor(out=t[:], in0=t[:], in1=rv[:], op=ALU.mult),
                             reads=[tb, rvb], writes=[tb])
                        j = c % 2
                        k.op(act, lambda: nc.scalar.activation(out=vb16[j][:], in_=t[:], func=AF.Identity,
                                                               bias=gcol("vb", c), scale=gcol("vg", c)),
                             reads=[tb, gc_b], writes=[vb16_b[j]])

                        def go():
                            b = next_bank()
                            pb16 = banks[b][:].bitcast(BF16)
                            fns = [(lambda tt=tt: nc.tensor.transpose(out=pb16[:, tt * 128:(tt + 1) * 128],
                                                                      in_=vb16[j][:, tt * 128:(tt + 1) * 128],
                                                                      identity=ident[:])) for tt in range(4)]
                            k.group(pe, fns, reads=[vb16_b[j], identb], writes=[bank_b[b]])
                            k.op(act, lambda: nc.scalar.copy(out=vln_v[:, :, c * 128:(c + 1) * 128],
                                                             in_=pb16[:, 0:512].rearrange("p (t c) -> p t c", t=4)),
                                 reads=[bank_b[b]], writes=[vln_b[c]])
                        defer_pe(go)

                    def ep_u(oc, b):
                        t, tb = work()
                        k.op(dve, lambda: nc.vector.tensor_tensor(out=t[:], in0=banks[b][:], in1=rx[:], op=ALU.mult),
                             reads=[bank_b[b], rxb], writes=[tb])
                        k.op(act, lambda: nc.scalar.activation(out=gnT[:, oc * 512:(oc + 1) * 512], in_=t[:],
                                                               func=AF.Gelu_apprx_tanh), reads=[tb], writes=[gn_b[oc]])
                        ln_apply(oc)
                    linear("u", GH, DC, rhs_x, rhsb_x, ep_u)
                    run_deferred()

                    rg, rgb = rep_tile(pb, "rg")
                    st = Stat(GH)
                    for g in range(GH):
                        b = next_bank()
                        fns = []
                        for tt in range(4):
                            fns.append(mm(banks[b][:, tt * 128:(tt + 1) * 128],
                                          vln_v[:, tt, g * 128:(g + 1) * 128], wsT[:, g * 128:(g + 1) * 128],
                                          True, False))
                            fns.append(mm(banks[b][:, tt * 128:(tt + 1) * 128], ones[0:2, :],
                                          bhl[0:2, g * 128:(g + 1) * 128], False, True))
                        k.group(pe, fns, reads=[vln_b[g], wsTb, bhlb, onesb], writes=[bank_b[b]])
                        k.op(dve, lambda: nc.vector.tensor_tensor(out=vt[:, g * 512:(g + 1) * 512], in0=banks[b][:],
                                                                  in1=gnT[:, g * 512:(g + 1) * 512], op=ALU.mult),
                             reads=[bank_b[b], gn_b[g]], writes=[vt_b[g]])
                        st.add_sq_of(vt[:, g * 512:(g + 1) * 512], [vt_b[g]])
                    rstd_from(st.bank, C.GW, rg[:], rgb)
                    st.done()
                    for g in range(GH):
                        k.op(dve, lambda: nc.vector.scalar_tensor_tensor(
                            out=gnT[:, g * 512:(g + 1) * 512], in0=vt[:, g * 512:(g + 1) * 512],
                            scalar=gcol("gn", g), in1=rg[:], op0=ALU.mult, op1=ALU.mult),
                            reads=[vt_b[g], rgb, gc_b], writes=[gn_b[g]])
                    retire(xg_b + xbuf_b + vt_b + vb16_b + vln_b + [rxb, meanb, rvb, rgb])

                pqk = ExitStack()
                QnT = sb(P_ + "QnT", [128, AH * 512], BF16, pqk)
                QnT_b = fresh([Buf(f"QnT{i}") for i in range(AH)])
                QrT = sb(P_ + "QrT", [128, AH * 512], BF16, pqk)
                QrT_b = fresh([Buf(f"QrT{i}") for i in range(AH)])
                k.op(dve, lambda: nc.vector.memset(QrT[64:128, :], 0.0), writes=QrT_b)
                perq = C.TB // QC
                assert perq % 2 == 0 and perq >= 2
                for h0 in range(0, AH, perq // 2):
                    wt, wb = wtile("uq")
                    for hl in range(min(perq // 2, AH - h0)):
                        h = h0 + hl
                        bn, bA, bB = next_bank(), next_bank(), next_bank()
                        for (bb, colofs, M) in ((bn, 0, 128), (bA, 128, 64), (bB, 192, 64)):
                            fns = []
                            for kc in range(QC):
                                c0 = ((hl * 2 + colofs // 128) * QC + kc) * 128 + (colofs % 128)
                                fns.append(mm(banks[bb][0:M, :], wt[:, c0:c0 + M], qn[:, kc * 512:(kc + 1) * 512],
                                              kc == 0, kc == QC - 1))
                            k.group(pe, fns, reads=[wb] + qn_b, writes=[bank_b[bb]])
                        k.op(act, lambda: nc.scalar.copy(out=QnT[:, h * 512:(h + 1) * 512], in_=banks[bn][:]),
                             reads=[bank_b[bn]], writes=[QnT_b[h]])
                        apply_rope(bA, bB, cos2, sin2, cs_b, None, QrT[0:64, h * 512:(h + 1) * 512], [QrT_b[h]])
                retire([cs_b] + qn_b)
                pcs.close()
                pqn.close()

                with ExitStack() as pd:
                    Kb = [sb(P_ + f"K{i}", [128, SEQ], BF16, pd) for i in range(2)]
                    Kb_b = fresh([Buf(f"K{i}") for i in range(2)])
                    Vb = [sb(P_ + f"V{i}", [128, SEQ], BF16, pd) for i in range(2)]
                    Vb_b = fresh([Buf(f"V{i}") for i in range(2)])
                    krT = sb(P_ + "krT", [128, SEQ], BF16, pd)
                    krT_b = fresh([Buf("krT")])[0]
                    k.op(dve, lambda: nc.vector.memset(krT[64:128, :], 0.0), writes=[krT_b])
                    NP = 4
                    PT = [sb(P_ + f"PT{i}", [128, 512], BF16, pd) for i in range(NP)]
                    PT_b = fresh([Buf(f"PT{i}") for i in range(NP)])
                    rz = [sb(P_ + f"rz{i}", [128, 512], F32, pd) for i in range(2)]
                    rz_b = fresh([Buf(f"rz{i}") for i in range(2)])
                    k.dma(sp, krT[0:64, :], kr_scr, krT_b, reads=[scr_guard], writes=[krT_b])

                    def load_kv(h):
                        j = h % 2
                        k.dma(sp, Kb[j][:], kT_scr[h], Kb_b[j], reads=[scr_guard], writes=[Kb_b[j]])
                        k.dma(sp, Vb[j][:], v_scr[h], Vb_b[j], reads=[scr_guard], writes=[Vb_b[j]])
                    load_kv(0)
                    zacc = [sb(P_ + f"zacc{i}", [128, 512], F32, pd) for i in range(4)]
                    zacc_b = fresh([Buf(f"zacc{i}") for i in range(4)])
                    accS = sb(P_ + "accS", [128, 512], F32, pd)
                    accS_b = fresh([Buf("accS")])[0]
                    run_deferred()
                    sbanks = reserve(3)
                    oz = reserve(4)
                    ptc = {"i": 0}
                    NKT = C.NKT
                    for h in range(AH):
                        j = h % 2
                        if h + 1 < AH:
                            load_kv(h + 1)
                        bO, bZ = oz[2 * j], oz[2 * j + 1]
                        zA, zAb, zB, zBb = zacc[2 * j], zacc_b[2 * j], zacc[2 * j + 1], zacc_b[2 * j + 1]

                        def emit_S(kb):
                            b = sbanks[kb % 3]
                            fns = [mm(banks[b][:], Kb[j][:, kb * 128:(kb + 1) * 128], QnT[:, h * 512:(h + 1) * 512],
                                      True, False),
                                   mm(banks[b][:], krT[:, kb * 128:(kb + 1) * 128], QrT[:, h * 512:(h + 1) * 512],
                                      False, True)]
                            k.group(pe, fns, reads=[Kb_b[j], QnT_b[h], QrT_b[h], krT_b], writes=[bank_b[b]])
                            p = ptc["i"] % NP
                            ptc["i"] += 1
                            k.op(act, lambda: nc.scalar.activation(out=PT[p][:], in_=banks[b][:], func=AF.Exp,
                                                                   scale=scale), reads=[bank_b[b]], writes=[PT_b[p]])
                            return p
                        pend = [emit_S(kb) for kb in range(min(2, NKT))]
                        run_deferred()
                        for kb in range(NKT):
                            if kb + 2 < NKT:
                                pend.append(emit_S(kb + 2))
                            p = pend.pop(0)
                            if kb == 3:
                                run_deferred()
                            k.group(pe, [mm(banks[bO][:], Vb[j][:, kb * 128:(kb + 1) * 128], PT[p][:], kb == 0,
                                            kb == NKT - 1)], reads=[Vb_b[j], PT_b[p]], writes=[bank_b[bO]])
                            za, zab = (zA, zAb) if kb % 2 == 0 else (zB, zBb)
                            ze, zr = (dve, nc.vector) if (kb % 2 == 0 or not C.POOL_Z) else (pool, nc.gpsimd)
                            if kb < 2:
                                k.op(ze, lambda: zr.tensor_copy(out=za[:], in_=PT[p][:]), reads=[PT_b[p]],
                                     writes=[zab])
                            else:
                                k.op(ze, lambda: zr.tensor_tensor(out=za[:], in0=za[:], in1=PT[p][:], op=ALU.add),
                                     reads=[zab, PT_b[p]], writes=[zab])

                        def head_end(h=h, j=j, bO=bO, bZ=bZ, zA=zA, zAb=zAb, zB=zB, zBb=zBb):
                            if NKT > 1:
                                k.op(dve, lambda: nc.vector.tensor_tensor(out=zA[:], in0=zA[:], in1=zB[:], op=ALU.add),
                                     reads=[zAb, zBb], writes=[zAb])
                            k.group(pe, [mm(banks[bZ][:], ones32[:], zA[:], True, True)], reads=[zAb, onesb],
                                    writes=[bank_b[bZ]])
                            if C.ACT_RECIP:
                                lz, lzb = work()
                                k.op(act, lambda: nc.scalar.activation(out=lz[:], in_=banks[bZ][:], func=AF.Ln),
                                     reads=[bank_b[bZ]], writes=[lzb])
                                k.op(act, lambda: nc.scalar.activation(out=rz[j][:], in_=lz[:], func=AF.Exp,
                                                                       scale=-1.0), reads=[lzb], writes=[rz_b[j]])
                            else:
                                recip(rz[j][:], banks[bZ][:], [bank_b[bZ]], [rz_b[j]])
                            t, tb = work()
                            k.op(dve, lambda: nc.vector.tensor_tensor(out=t[:], in0=banks[bO][:], in1=rz[j][:],
                                                                      op=ALU.mult),
                                 reads=[bank_b[bO], rz_b[j]], writes=[tb])
                            if h == 0:
                                k.op(act, lambda: nc.scalar.activation(out=accS[:], in_=t[:], func=AF.Square),
                                     reads=[tb], writes=[accS_b])
                            else:
                                s32, s32b = work()
                                k.op(act, lambda: nc.scalar.activation(out=s32[:], in_=t[:], func=AF.Square),
                                     reads=[tb], writes=[s32b])
                                k.op(dve, lambda: nc.vector.tensor_tensor(out=accS[:], in0=accS[:], in1=s32[:],
                                                                          op=ALU.add),
                                     reads=[accS_b, s32b], writes=[accS_b])
                            k.op(dve, lambda: nc.vector.tensor_scalar(out=anT[:, h * 512:(h + 1) * 512], in0=t[:],
                                                                      scalar1=gcol("an", h), scalar2=None,
                                                                      op0=ALU.mult),
                                 reads=[tb, gc_b], writes=[an_b[h]])
                        defer_pe(head_end)
                    run_deferred()
                    release(sbanks)
                    release(oz)
                    bS = next_bank()
                    k.group(pe, [mm(banks[bS][:], ones32[:], accS[:], True, True)], reads=[accS_b, onesb],
                            writes=[bank_b[bS]])
                    rstd_from(bS, C.AW, ra[:], rab)
                    retire(Kb_b + Vb_b + [krT_b] + PT_b + rz_b + QnT_b + QrT_b + zacc_b + [accS_b])
                pqk.close()

                with ExitStack() as pe_:
                    mo = sb(P_ + "mo", [128, DC * 512], F32, pe_)
                    mo_b = fresh([Buf(f"mo{i}") for i in range(DC)])
                    xbuf = [sb(P_ + f"Exbuf{i}", [128, XG, 512], F32, pe_) for i in range(2)]
                    xbuf_b = fresh([Buf(f"Exbuf{i}") for i in range(2)])
                    rp, rpb = rep_tile(pe_, "rp")
                    rh, rhb = rep_tile(pe_, "rh")
                    st = Stat(DC)
                    for oc in range(DC):
                        wt, wb = wtile("out")
                        bA, bG = next_bank(), next_bank()
                        fns = [mm(banks[bA][:], wt[:, kc * 128:(kc + 1) * 128], anT[:, kc * 512:(kc + 1) * 512],
                                  kc == 0, kc == AH - 1) for kc in range(AH)]
                        k.group(pe, fns, reads=[wb] + an_b, writes=[bank_b[bA]])
                        fns = [mm(banks[bG][:], wt[:, (AH + kc) * 128:(AH + kc + 1) * 128],
                                  gnT[:, kc * 512:(kc + 1) * 512], kc == 0, kc == GH - 1) for kc in range(GH)]
                        k.group(pe, fns, reads=[wb] + gn_b, writes=[bank_b[bG]])
                        t, tb = work()
                        k.op(dve, lambda: nc.vector.tensor_tensor(out=t[:], in0=banks[bA][:], in1=ra[:], op=ALU.mult),
                             reads=[bank_b[bA], rab], writes=[tb])
                        sl = slice(oc * 512, (oc + 1) * 512)
                        k.op(dve, lambda: nc.vector.tensor_tensor(out=mo[:, sl], in0=t[:], in1=banks[bG][:],
                                                                  op=ALU.add),
                             reads=[tb, bank_b[bG]], writes=[mo_b[oc]])
                        st.add_sq_of(mo[:, sl], [mo_b[oc]])
                    rstd_from(st.bank, C.D, rp[:], rpb)
                    st.done()
                    st = Stat(DC)

                    def load_x(gi):
                        j = gi % 2
                        src = xT[gi * XG * 128:(gi + 1) * XG * 128, tok0:tok0 + 512].rearrange(
                            "(g p) t -> p g t", p=128)
                        k.dma(sp, xbuf[j][:], src, xbuf_b[j], writes=[xbuf_b[j]])
                        return xbuf[j], xbuf_b[j]
                    ng = DC // XG
                    nxt = load_x(0)
                    for gi in range(ng):
                        xb_, xbb = nxt
                        if gi + 1 < ng:
                            nxt = load_x(gi + 1)
                        for g in range(XG):
                            d = gi * XG + g
                            sl = slice(d * 512, (d + 1) * 512)
                            k.op(dve, lambda: nc.vector.scalar_tensor_tensor(
                                out=mo[:, sl], in0=mo[:, sl], scalar=gcol("pm", d), in1=rp[:], op0=ALU.mult,
                                op1=ALU.mult), reads=[mo_b[d], rpb, gc_b], writes=[mo_b[d]])
                            k.op(dve, lambda: nc.vector.tensor_tensor(out=mo[:, sl], in0=mo[:, sl], in1=xb_[:, g, :],
                                                                      op=ALU.add),
                                 reads=[mo_b[d], xbb], writes=[mo_b[d]])
                            k.dma(sp, h_scr[d][:, tok0:tok0 + T], mo[:, sl], mo_b[d], reads=[mo_b[d]])
                            st.add_sq_of(mo[:, sl], [mo_b[d]])
                    rstd_from(st.bank, C.D, rh[:], rhb)
                    st.done()
                    hn_b = an_b + gn_b

                    def hn_ap(d):
                        if d < AH:
                            return anT[:, d * 512:(d + 1) * 512]
                        return gnT[:, (d - AH) * 512:(d - AH + 1) * 512]
                    for d in range(DC):
                        sl = slice(d * 512, (d + 1) * 512)
                        k.op(dve, lambda: nc.vector.scalar_tensor_tensor(
                            out=hn_ap(d), in0=mo[:, sl], scalar=gcol("pf", d), in1=rh[:], op0=ALU.mult,
                            op1=ALU.mult), reads=[mo_b[d], rhb, gc_b], writes=[hn_b[d]])
                    retire(mo_b + xbuf_b + [rpb, rhb, rab])

                rf, rfb = rep_tile(ph, "rf")
                with ExitStack() as pf:
                    actT = sb(P_ + "act", [128, FC * 512], BF16, pf)
                    act_b = fresh([Buf(f"act{i}") for i in range(FC)])
                    fstage = [sb(P_ + f"fs{i}", [128, 512], F32, pf) for i in range(3)]
                    fstage_b = fresh([Buf(f"fs{i}") for i in range(3)])
                    for fc in range(FC):
                        bg, bu = next_bank(), next_bank()
                        for bb in (bg, bu):
                            wt, wb = wtile("gu")
                            fns = [mm(banks[bb][:], wt[:, kc * 128:(kc + 1) * 128], hn_ap(kc), kc == 0, kc == DC - 1)
                                   for kc in range(DC)]
                            k.group(pe, fns, reads=[wb] + hn_b, writes=[bank_b[bb]])
                        t, tb = work()
                        k.op(act, lambda: nc.scalar.activation(out=t[:], in_=banks[bg][:], func=AF.Silu),
                             reads=[bank_b[bg]], writes=[tb])
                        k.op(dve, lambda: nc.vector.tensor_tensor(out=actT[:, fc * 512:(fc + 1) * 512], in0=t[:],
                                                                  in1=banks[bu][:], op=ALU.mult),
                             reads=[tb, bank_b[bu]], writes=[act_b[fc]])
                    st = Stat(DC)

                    def ep_d(oc, b):
                        j = oc % 3
                        k.op(act, lambda: nc.scalar.copy(out=fstage[j][:], in_=banks[b][:]), reads=[bank_b[b]],
                             writes=[fstage_b[j]])
                        st.add_sq_of(fstage[j][:], [fstage_b[j]])
                        k.dma(sp, f_scr[oc][:, tok0:tok0 + T], fstage[j][:], fstage_b[j], reads=[fstage_b[j]])
                    linear("down", DC, FC, lambda kc: actT[:, kc * 512:(kc + 1) * 512],
                           lambda s, n: act_b[s:s + n], ep_d)
                    rstd_from(st.bank, C.D, rf[:], rfb)
                    st.done()
                    retire(act_b + fstage_b)
                with ExitStack() as pfin:
                    FG = 4
                    NF = 3
                    finf = [sb(P_ + f"finf{i}", [128, FG, 512], F32, pfin) for i in range(NF)]
                    finf_b = fresh([Buf(f"finf{i}") for i in range(NF)])
                    finh = [sb(P_ + f"finh{i}", [128, FG, 512], F32, pfin) for i in range(NF)]
                    finh_b = fresh([Buf(f"finh{i}") for i in range(NF)])
                    for (sem, val) in DEAD.values():
                        k.wait(sp, (sem, val))
                    ngf = DC // FG

                    def load_fin(gi):
                        j = gi % NF
                        d0 = gi * FG
                        k.dma(sp, finf[j][:], f_scr[d0:d0 + FG, :, tok0:tok0 + T].rearrange("d p t -> p d t"),
                              finf_b[j], writes=[finf_b[j]])
                        k.dma(sp, finh[j][:], h_scr[d0:d0 + FG, :, tok0:tok0 + T].rearrange("d p t -> p d t"),
                              finh_b[j], writes=[finh_b[j]])
                    for gi in range(min(2, ngf)):
                        load_fin(gi)
                    for gi in range(ngf):
                        j = gi % NF
                        if gi + 2 < ngf:
                            load_fin(gi + 2)
                        for c in range(FG):
                            d = gi * FG + c
                            k.op(dve, lambda: nc.vector.scalar_tensor_tensor(
                                out=finf[j][:, c, :], in0=finf[j][:, c, :], scalar=gcol("po", d), in1=rf[:],
                                op0=ALU.mult, op1=ALU.mult), reads=[finf_b[j], rfb, gc_b], writes=[finf_b[j]])
                            k.op(dve, lambda: nc.vector.tensor_tensor(out=finf[j][:, c, :], in0=finf[j][:, c, :],
                                                                      in1=finh[j][:, c, :], op=ALU.add),
                                 reads=[finf_b[j], finh_b[j]], writes=[finf_b[j]])
                        d0 = gi * FG
                        k.dma(sp, outT[d0 * 128:(d0 + FG) * 128, tok0:tok0 + T].rearrange("(d p) t -> p d t", p=128),
                              finf[j][:], finf_b[j], reads=[finf_b[j]])
                    final_bufs = list(finf_b)
                    retire(finf_b + finh_b + [rfb])
                retire(gn_b + an_b)
        for b_ in final_bufs:
            k.wait(sp, b_.dlast)
        print(f"[kernel] arena peak={arena.peak/1024:.1f} KiB")
        print(f"[kernel] semaphores={k.nsem} waits={k.nwait} pe_groups={k.pe.cnt} act={k.act.cnt} dve={k.dve.cnt}")
    return nc


def host_inputs(C, x_b, pos_b, p, quarter):
    rot = quarter * C.OWN
    xr = np.roll(x_b, -rot, axis=0)
    pr = np.roll(pos_b, -rot, axis=0)
    xT = np.ascontiguousarray(xr.T)
    gc = np.zeros((128, 512), np.float32)
    o = 0
    for nm, n in (("pre_mix_norm", C.DC), ("q_norm", C.QC), ("kv_norm", C.KC), ("v_ln_gain", C.GH),
                  ("v_ln_bias", C.GH), ("attn_out_norm", C.AH), ("gmlp_out_norm", C.GH), ("post_mix_norm", C.DC),
                  ("pre_ffn_norm", C.DC), ("post_ffn_norm", C.DC)):
        gc[:, o:o + n] = p[nm].reshape(n, 128).T
        o += n
    return {"xT": xT, "pos": pr.reshape(1, -1).astype(np.int32), "gcols": gc}


def shared_inputs(C, p):
    wsT = np.ascontiguousarray(p["w_spatial"].transpose(2, 0, 1)).reshape(128, C.GH * 128)
    bsp = p["b_spatial"].reshape(C.GH, 128).astype(np.float32)
    inv = (1.0 / (10000.0 ** (np.arange(0, 64, 2, dtype=np.float32) / np.float32(64)))).astype(np.float32)
    rc = np.zeros((64, 4), np.float32)
    rc[:, 0] = np.concatenate([inv, inv])
    rc[:, 1] = np.concatenate([-np.ones(32, np.float32), np.ones(32, np.float32)])
    ident = np.eye(128, dtype=np.float32)
    return {"wsT": wsT.astype(np.float32), "bsp": bsp, "ropec": rc, "ident": ident}


_CACHE = {}


def run(C, x, positions, p, n_cores, debug=False):
    wstream, windex = build_wstream(C, p["w_in"], p["w_uq"], p["w_ukv"], p["w_out"], p["w_gate"], p["w_up"],
                                    p["w_down"])
    key = (C.D, C.SEQ, C.DFF, debug)
    if key not in _CACHE:
        _CACHE[key] = build_program(C, windex, debug=debug)
    nc = _CACHE[key]
    sh = shared_inputs(C, p)
    in_maps = []
    qpb = C.SEQ // C.OWN
    for c in range(n_cores):
        b, q = c // qpb, c % qpb
        m = host_inputs(C, x[b], positions[b], p, q)
        m.update(sh)
        m["wstream"] = wstream
        in_maps.append(m)
    res = run_bass_kernel_spmd(nc, in_maps, core_ids=list(range(n_cores)))
    out = np.empty(x.shape, np.float32)
    for c in range(n_cores):
        b, q = c // qpb, c % qpb
        out[b, q * C.OWN:(q + 1) * C.OWN, :] = res.results[c]["outT"].T
    return out, res


def kernel(x, positions, pre_mix_norm, w_in, q_norm, kv_norm, w_uq, w_ukv, v_ln_gain, v_ln_bias, w_spatial,
           b_spatial, attn_out_norm, gmlp_out_norm, w_out, post_mix_norm, pre_ffn_norm, w_gate, w_up, w_down,
           post_ffn_norm):
    C = Cfg()
    loc = dict(locals())
    p = {}
    for nm in ("pre_mix_norm", "w_in", "q_norm", "kv_norm", "w_uq", "w_ukv", "v_ln_gain", "v_ln_bias", "w_spatial",
               "b_spatial", "attn_out_norm", "gmlp_out_norm", "w_out", "post_mix_norm", "pre_ffn_norm", "w_gate",
               "w_up", "w_down", "post_ffn_norm"):
        p[nm] = np.asarray(loc[nm], dtype=np.float32)[0]
    x = np.asarray(x, dtype=np.float32)
    positions = np.asarray(positions)
    out, _ = run(C, x, positions, p, 8)
    return out
```
